# Optimizing a Trainium2 kernel written in Bass

```python
import math
import jax, jax.numpy as jnp
from jax import lax
import numpy as np

D_MODEL = 1024
BATCH = 8
SEQ = 4096
DEPTH = 2

BLOCK_Q = 128
RMS_EPS = 1e-6
SB_HEADS = 8
SB_HEAD_DIM = 64
SB_WIDTH = SB_HEADS * SB_HEAD_DIM
MLA_HEADS = 8
MLA_NOPE_DIM = 64
MLA_ROPE_DIM = 32
MLA_V_DIM = 64
MLA_Q_LORA = 384
MLA_KV_LORA = 256
MLA_WIDTH = MLA_HEADS * MLA_V_DIM
ROPE_THETA = 10000.0
FOX_HEADS = 16
FOX_HEAD_DIM = 64
FOX_WIDTH = FOX_HEADS * FOX_HEAD_DIM
EVEN_IN_WIDTH = 4 * SB_WIDTH + MLA_Q_LORA + MLA_KV_LORA + MLA_ROPE_DIM + MLA_WIDTH
ODD_IN_WIDTH = 4 * FOX_WIDTH + FOX_HEADS

kernel_name = "hybrid_stickbreak_mla_fox_sandwich"


def rms_norm(x, g):
    xf = x.astype(jnp.float32)
    var = jnp.mean(xf * xf, axis=-1, keepdims=True)
    return (xf * lax.rsqrt(var + RMS_EPS)).astype(x.dtype) * g


def split_heads(t, n_heads):
    b, s, _ = t.shape
    return t.reshape(b, s, n_heads, -1).transpose(0, 2, 1, 3)


def merge_heads(o):
    b, h, s, d = o.shape
    return o.transpose(0, 2, 1, 3).reshape(b, s, h * d)


def sweep_blocks(block_fn, n_blocks):
    out = lax.map(block_fn, jnp.arange(n_blocks))
    nb, b, h, bq, d = out.shape
    return out.transpose(1, 2, 0, 3, 4).reshape(b, h, nb * bq, d)


def rope_angles(positions, dim):
    inv_freq = ROPE_THETA ** (-jnp.arange(0, dim, 2, dtype=jnp.float32) / dim)
    ang = positions.astype(jnp.float32)[..., None] * inv_freq
    return jnp.cos(ang), jnp.sin(ang)


def apply_rope(x, cos, sin):
    x1, x2 = jnp.split(x, 2, axis=-1)
    cos = cos.astype(x.dtype)
    sin = sin.astype(x.dtype)
    return jnp.concatenate([x1 * cos - x2 * sin, x2 * cos + x1 * sin], axis=-1)


def stick_breaking_attention(q, k, v):
    s_len, d = q.shape[2], q.shape[3]
    scale = d ** -0.5
    k_pos = jnp.arange(s_len)

    def one_block(i):
        start = i * BLOCK_Q
        qb = lax.dynamic_slice_in_dim(q, start, BLOCK_Q, axis=2)
        z = jnp.einsum('bhqd,bhkd->bhqk', qb, k).astype(jnp.float32) * scale
        q_pos = start + jnp.arange(BLOCK_Q)
        before = k_pos[None, :] < q_pos[:, None]
        log_keep = jnp.where(before, jax.nn.log_sigmoid(-z), 0.0)
        log_remain = lax.cumsum(log_keep, axis=3, reverse=True) - log_keep
        w = jnp.where(before, jnp.exp(jax.nn.log_sigmoid(z) + log_remain), 0.0)
        return jnp.einsum('bhqk,bhkd->bhqd', w.astype(v.dtype), v)

    return sweep_blocks(one_block, s_len // BLOCK_Q)


def mla_attention(q_nope, q_rope, k_nope, k_rope, v):
    s_len = q_nope.shape[2]
    scale = (MLA_NOPE_DIM + MLA_ROPE_DIM) ** -0.5
    k_pos = jnp.arange(s_len)

    def one_block(i):
        start = i * BLOCK_Q
        qn = lax.dynamic_slice_in_dim(q_nope, start, BLOCK_Q, axis=2)
        qr = lax.dynamic_slice_in_dim(q_rope, start, BLOCK_Q, axis=2)
        z = (jnp.einsum('bhqd,bhkd->bhqk', qn, k_nope)
             + jnp.einsum('bhqr,bkr->bhqk', qr, k_rope)).astype(jnp.float32) * scale
        q_pos = start + jnp.arange(BLOCK_Q)
        causal = k_pos[None, :] <= q_pos[:, None]
        p = jax.nn.softmax(jnp.where(causal, z, -jnp.inf), axis=-1)
        return jnp.einsum('bhqk,bhkd->bhqd', p.astype(v.dtype), v)

    return sweep_blocks(one_block, s_len // BLOCK_Q)


def forgetting_attention(q, k, v, log_f):
    s_len, d = q.shape[2], q.shape[3]
    scale = d ** -0.5
    c = lax.cumsum(log_f, axis=2)
    k_pos = jnp.arange(s_len)

    def one_block(i):
        start = i * BLOCK_Q
        qb = lax.dynamic_slice_in_dim(q, start, BLOCK_Q, axis=2)
        cq = lax.dynamic_slice_in_dim(c, start, BLOCK_Q, axis=2)
        z = jnp.einsum('bhqd,bhkd->bhqk', qb, k).astype(jnp.float32) * scale
        z = z + cq[..., :, None] - c[..., None, :]
        q_pos = start + jnp.arange(BLOCK_Q)
        causal = k_pos[None, :] <= q_pos[:, None]
        p = jax.nn.softmax(jnp.where(causal, z, -jnp.inf), axis=-1)
        return jnp.einsum('bhqk,bhkd->bhqd', p.astype(v.dtype), v)

    return sweep_blocks(one_block, s_len // BLOCK_Q)


def even_layer(x, positions, pre_g, post_g, w_in, q_a_g, w_q_b, kv_a_g, w_kv_b, w_out):
    b, s, _ = x.shape
    h = rms_norm(x, pre_g)
    proj = h @ w_in
    cuts = [SB_WIDTH, 2 * SB_WIDTH, 3 * SB_WIDTH, 4 * SB_WIDTH,
            4 * SB_WIDTH + MLA_Q_LORA,
            4 * SB_WIDTH + MLA_Q_LORA + MLA_KV_LORA + MLA_ROPE_DIM]
    sb_q, sb_k, sb_v, sb_gate, q_a, kv_a, mla_gate = jnp.split(proj, cuts, axis=-1)

    o_a = stick_breaking_attention(split_heads(sb_q, SB_HEADS), split_heads(sb_k, SB_HEADS),
                                   split_heads(sb_v, SB_HEADS))
    o_a = merge_heads(o_a) * jax.nn.silu(sb_gate)

    q = (rms_norm(q_a, q_a_g) @ w_q_b).reshape(b, s, MLA_HEADS, MLA_NOPE_DIM + MLA_ROPE_DIM)
    q = q.transpose(0, 2, 1, 3)
    q_nope, q_rope = q[..., :MLA_NOPE_DIM], q[..., MLA_NOPE_DIM:]
    c_kv, k_rope = kv_a[..., :MLA_KV_LORA], kv_a[..., MLA_KV_LORA:]
    kv = (rms_norm(c_kv, kv_a_g) @ w_kv_b).reshape(b, s, MLA_HEADS, MLA_NOPE_DIM + MLA_V_DIM)
    kv = kv.transpose(0, 2, 1, 3)
    k_nope, v = kv[..., :MLA_NOPE_DIM], kv[..., MLA_NOPE_DIM:]
    cos, sin = rope_angles(positions, MLA_ROPE_DIM)
    q_rope = apply_rope(q_rope, cos[:, None], sin[:, None])
    k_rope = apply_rope(k_rope, cos, sin)
    o_b = mla_attention(q_nope, q_rope, k_nope, k_rope, v)
    o_b = merge_heads(o_b) * jax.nn.silu(mla_gate)

    y = jnp.concatenate([o_a, o_b], axis=-1) @ w_out
    return x + rms_norm(y, post_g)


def odd_layer(x, pre_g, post_g, w_in, b_f, w_out):
    h = rms_norm(x, pre_g)
    proj = h @ w_in
    cuts = [FOX_WIDTH, 2 * FOX_WIDTH, 3 * FOX_WIDTH, 4 * FOX_WIDTH]
    q, k, v, gate, f_logit = jnp.split(proj, cuts, axis=-1)
    log_f = jax.nn.log_sigmoid(f_logit.astype(jnp.float32) + b_f.astype(jnp.float32))
    log_f = log_f.transpose(0, 2, 1)
    o = forgetting_attention(split_heads(q, FOX_HEADS), split_heads(k, FOX_HEADS),
                             split_heads(v, FOX_HEADS), log_f)
    y = (merge_heads(o) * jax.nn.silu(gate)) @ w_out
    return x + rms_norm(y, post_g)


def setup_inputs(seed: int = 0) -> dict:
    key = jax.random.key(seed)
    ks = jax.random.split(key, 20)

    def w(k, shape):
        return jax.random.normal(k, shape, jnp.float32) * (shape[0] ** -0.5)

    def gain(k, n):
        return 1.0 + 0.02 * jax.random.normal(k, (n,), jnp.float32)

    x = jax.random.normal(ks[0], (BATCH, SEQ, D_MODEL), jnp.float32)
    positions = jnp.broadcast_to(jnp.arange(SEQ, dtype=jnp.int32)[None, :], (BATCH, SEQ))
    return {
        "x": x,
        "positions": positions,
        "l0_pre_g": gain(ks[1], D_MODEL),
        "l0_post_g": gain(ks[2], D_MODEL),
        "l0_w_in": w(ks[3], (D_MODEL, EVEN_IN_WIDTH)),
        "l0_q_a_g": gain(ks[4], MLA_Q_LORA),
        "l0_w_q_b": w(ks[5], (MLA_Q_LORA, MLA_HEADS * (MLA_NOPE_DIM + MLA_ROPE_DIM))),
        "l0_kv_a_g": gain(ks[6], MLA_KV_LORA),
        "l0_w_kv_b": w(ks[7], (MLA_KV_LORA, MLA_HEADS * (MLA_NOPE_DIM + MLA_V_DIM))),
        "l0_w_out": w(ks[8], (SB_WIDTH + MLA_WIDTH, D_MODEL)),
        "l1_pre_g": gain(ks[9], D_MODEL),
        "l1_post_g": gain(ks[10], D_MODEL),
        "l1_w_in": w(ks[11], (D_MODEL, ODD_IN_WIDTH)),
        "l1_b_f": 2.0 + 0.5 * jax.random.normal(ks[12], (FOX_HEADS,), jnp.float32),
        "l1_w_out": w(ks[13], (FOX_WIDTH, D_MODEL)),
    }


def reference(x, positions, l0_pre_g, l0_post_g, l0_w_in, l0_q_a_g, l0_w_q_b, l0_kv_a_g,
              l0_w_kv_b, l0_w_out, l1_pre_g, l1_post_g, l1_w_in, l1_b_f, l1_w_out):
    layer_params = [
        (l0_pre_g, l0_post_g, l0_w_in, l0_q_a_g, l0_w_q_b, l0_kv_a_g, l0_w_kv_b, l0_w_out),
        (l1_pre_g, l1_post_g, l1_w_in, l1_b_f, l1_w_out),
    ]
    for layer in range(DEPTH):
        p = layer_params[layer]
        if layer % 2 == 0:
            x = even_layer(x, positions, *p)
        else:
            x = odd_layer(x, *p)
    return x
```

```python
from contextlib import ExitStack
import numpy as np
import concourse.bass as bass
import concourse.mybir as mybir
from concourse.bass_utils import run_bass_kernel_spmd

F32 = mybir.dt.float32
BF16 = mybir.dt.bfloat16
I32 = mybir.dt.int32
AF = mybir.ActivationFunctionType
ALU = mybir.AluOpType

S_LEN = 4096
D = 1024
NB = 32
NT = 8
EPS = 1e-6
QW = 1024
SEM_ROLL = 1000


class Res:
    __slots__ = ("name", "last_write", "readers")

    def __init__(self, name=""):
        self.name = name
        self.last_write = None
        self.readers = []


class Op:
    __slots__ = ("eng", "fn", "deps", "is_dma", "ndma", "signal", "sem", "val", "idx", "barrier")

    def __init__(self, eng, fn, ndma):
        self.eng = eng
        self.fn = fn
        self.deps = []
        self.is_dma = ndma > 0
        self.ndma = ndma
        self.signal = False
        self.sem = None
        self.val = 0
        self.idx = 0
        self.barrier = False


class Sched:
    def __init__(self, nc, n_dma_sems=8):
        self.nc = nc
        self.engs = {"pe": nc.tensor, "act": nc.scalar, "dve": nc.vector, "pool": nc.gpsimd, "sp": nc.sync}
        self.ops = []
        self.n_dma_sems = n_dma_sems

    def op(self, eng, fn, reads=(), writes=(), dma=0):
        o = Op(eng, fn, dma)
        o.idx = len(self.ops)
        deps = {}
        for r in reads:
            if r.last_write is not None:
                deps[r.last_write.idx] = r.last_write
        for w in writes:
            if w.last_write is not None:
                deps[w.last_write.idx] = w.last_write
            for rd in w.readers:
                deps[rd.idx] = rd
        for r in reads:
            r.readers.append(o)
        for w in writes:
            w.last_write = o
            w.readers = []
        o.deps = list(deps.values())
        self.ops.append(o)
        return o

    def barrier(self):
        for e in self.engs:
            o = Op(e, None, 0)
            o.idx = len(self.ops)
            o.barrier = True
            self.ops.append(o)

    def emit(self):
        nc = self.nc
        last_on = {}
        for o in self.ops:
            if o.barrier:
                for e, lo in last_on.items():
                    lo.signal = True
                continue
            best = {}
            dmas = []
            for d in o.deps:
                if d.is_dma:
                    dmas.append(d)
                else:
                    if d.eng == "pe" and o.eng == "pe":
                        continue
                    b = best.get(d.eng)
                    if b is None or d.idx > b.idx:
                        best[d.eng] = d
            o.deps = list(best.values()) + dmas
            for d in o.deps:
                d.signal = True
            if not o.is_dma:
                last_on[o.eng] = o
        tl = {}
        all_tl = []
        dma_pool = {}
        waited = {}
        nsem = [0]

        def new_sem(name):
            nsem[0] += 1
            return nc.alloc_semaphore(name=f"{name}_{nsem[0]}")

        def do_waits(engine, wd, waits):
            mw = {}
            for s, v in waits:
                k = id(s)
                if wd.get(k, 0) >= v:
                    continue
                if k not in mw or mw[k][1] < v:
                    mw[k] = (s, v)
            for k, (s, v) in mw.items():
                engine.wait_ge(s, v)
                wd[k] = v

        for o in self.ops:
            engine = self.engs[o.eng]
            wd = waited.setdefault(o.eng, {})
            if o.barrier:
                waits = [(t[0], t[1]) for t in all_tl if t[1] > 0]
                for pool in dma_pool.values():
                    for ent in pool[0]:
                        if ent is not None and ent[1] > 0:
                            waits.append((ent[0], ent[1]))
                    for ent in pool[2]:
                        waits.append((ent[0], ent[1]))
                do_waits(engine, wd, waits)
                continue
            waits = [(d.sem, d.val) for d in o.deps]
            if o.is_dma:
                pool = dma_pool.get(o.eng)
                if pool is None:
                    pool = [[None] * self.n_dma_sems, 0, []]
                    dma_pool[o.eng] = pool
                slot = pool[1] % self.n_dma_sems
                pool[1] += 1
                ent = pool[0][slot]
                if ent is not None and ent[1] > 0:
                    waits.append((ent[0], ent[1]))
                if ent is None or ent[1] + 16 * o.ndma >= SEM_ROLL:
                    if ent is not None:
                        pool[2].append(ent)
                    ent = [new_sem(f"dma_{o.eng}_{slot}"), 0]
                    pool[0][slot] = ent
                o.sem = ent[0]
                ent[1] += 16 * o.ndma
                o.val = ent[1]
                do_waits(engine, wd, waits)
                insts = o.fn()
                if not isinstance(insts, (list, tuple)):
                    insts = [insts]
                assert len(insts) == o.ndma, (len(insts), o.ndma)
                for ins in insts:
                    ins.then_inc(o.sem, 16)
            else:
                do_waits(engine, wd, waits)
                ins = o.fn()
                if isinstance(ins, (list, tuple)):
                    ins = ins[-1]
                if o.signal:
                    t = tl.get(o.eng)
                    if t is None or t[1] >= SEM_ROLL:
                        t = [new_sem(f"tl_{o.eng}"), 0]
                        tl[o.eng] = t
                        all_tl.append(t)
                    t[1] += 1
                    o.sem = t[0]
                    o.val = t[1]
                    ins.then_inc(o.sem, 1)
        for e, engine in self.engs.items():
            wd = waited.setdefault(e, {})
            waits = [(t[0], t[1]) for t in all_tl if t[1] > 0]
            for pool in dma_pool.values():
                for ent in pool[0]:
                    if ent is not None and ent[1] > 0:
                        waits.append((ent[0], ent[1]))
                for ent in pool[2]:
                    waits.append((ent[0], ent[1]))
            do_waits(engine, wd, waits)


def build_nc(layers=(0, 1), debug=False):
    nc = bass.Bass("TRN2", target_bir_lowering=False)
    S = Sched(nc)

    def din(name, shape, dt=F32):
        return nc.dram_tensor(name, list(shape), dt, kind="ExternalInput").ap()

    def dscr(name, shape, dt=BF16):
        kind = "ExternalOutput" if debug else "Internal"
        return nc.dram_tensor(name, list(shape), dt, kind=kind).ap()

    x_in = din("x", [S_LEN, D])
    pos_in = din("pos", [128, NB], I32)
    consts_in = din("consts", [128, 8 * 128])
    g0_pre = din("l0_pre_g", [128, 8])
    g0_post = din("l0_post_g", [D])
    w0_in = din("l0_w_in", [D, 3232])
    g0_qa = din("l0_q_a_g", [128, 3])
    w0_qb = din("l0_w_q_b", [384, 768])
    g0_kva = din("l0_kv_a_g", [128, 2])
    w0_kvb = din("l0_w_kv_b", [256, 1024])
    w0_out = din("l0_w_out", [D, D])
    g1_pre = din("l1_pre_g", [128, 8])
    g1_post = din("l1_post_g", [D])
    w1_in = din("l1_w_in", [D, 4112])
    b1_f = din("l1_b_f", [16])
    w1_out = din("l1_w_out", [D, D])
    y_out = nc.dram_tensor("y", [S_LEN, D], F32, kind="ExternalOutput").ap()

    X1 = dscr("X1", [S_LEN, D], F32)
    SQK = dscr("SQK", [8, 128, S_LEN])
    SG = dscr("SG", [8, 128, S_LEN])
    SV = dscr("SV", [S_LEN, 512])
    MQ = dscr("MQ", [8, 96, S_LEN])
    MK = dscr("MK", [8, 64, S_LEN])
    MKR = dscr("MKR", [32, S_LEN])
    MV = dscr("MV", [S_LEN, 512])
    OGT = dscr("OGT", [8, 128, S_LEN])
    FQ = dscr("FQ", [16, 65, S_LEN])
    FK = dscr("FK", [16, 64, S_LEN])
    FG = dscr("FG", [8, 128, S_LEN])
    FV = dscr("FV", [S_LEN, 1024])

    es = ExitStack()

    def sb(name, shape, dt, stack=None):
        return (stack or es).enter_context(nc.sbuf_tensor(name, list(shape), dt))

    def ps(name, shape, dt, stack):
        return stack.enter_context(nc.psum_tensor(name, list(shape), dt))

    cst_f = sb("cst_f", [128, 1024], F32)
    cst_b = sb("cst_b", [128, 1024], BF16)
    ident_b = cst_b[:, 0:128]
    tri_b = cst_b[:, 128:256]
    onesn_b = cst_b[:, 256:384]
    mle_b = cst_b[:, 384:512]
    mlt_b = cst_b[:, 512:640]
    ident_f = cst_f[:, 0:128]
    triC_f = cst_f[:, 640:768]
    ones_f = cst_f[:, 768:896]
    wout_b = sb("wout_b", [128, 2, 8, D], BF16)
    gpost = sb("gpost", [128, 2, D], F32)
    negc = sb("negc", [128, NB, 16], F32)
    mhalf = sb("mhalf", [128, 1], F32)
    R_cst = Res("cst")
    R_wout = Res("wout")
    R_gpost = Res("gpost")
    R_negc = Res("negc")
    R_mhalf = Res("mhalf")

    def dma(eng, out, in_, reads=(), writes=()):
        e = nc.sync if eng == "sp" else nc.gpsimd
        return S.op(eng, lambda: e.dma_start(out=out, in_=in_), reads=reads, writes=writes, dma=1)

    def dmas(eng, pairs, reads=(), writes=()):
        e = nc.sync if eng == "sp" else nc.gpsimd
        return S.op(eng, lambda: [e.dma_start(out=o, in_=i) for o, i in pairs], reads=reads, writes=writes,
                    dma=len(pairs))

    def mms(specs, reads, writes):
        def f():
            ins = None
            for sp4 in specs:
                (o, l, r, st, sp_) = sp4[:5]
                if len(sp4) > 5:
                    ins = nc.tensor.matmul(o, lhsT=l, rhs=r, start=st, stop=sp_, skip_group_check=True)
                else:
                    ins = nc.tensor.matmul(o, lhsT=l, rhs=r, start=st, stop=sp_)
            return ins
        return S.op("pe", f, reads=reads, writes=writes)

    def transposes(specs, reads, writes):
        def f():
            ins = None
            for (o, i, idn) in specs:
                ins = nc.tensor.transpose(o, i, idn)
            return ins
        return S.op("pe", f, reads=reads, writes=writes)

    def V(eng, method, kw, reads, writes):
        engine = S.engs[eng]
        return S.op(eng, lambda: getattr(engine, method)(**kw), reads=reads, writes=writes)

    def SQ(junk, R_jk, jkc, cols, kw, reads, writes):
        j = jkc[0] % 2
        jkc[0] += 1
        if cols == ":":
            o = junk[:, j, :]
        else:
            a, b_ = cols.split(":")
            o = junk[:, j, int(a):int(b_)]
        kw = dict(kw)
        kw["out"] = o
        return V("act", "activation", kw, reads, list(writes) + [R_jk[j]])

    dma("sp", cst_f[:, :], consts_in[:, :], writes=[R_cst])
    V("dve", "tensor_copy", dict(out=cst_b[:, :], in_=cst_f[:, :]), [R_cst], [R_cst])
    V("pool", "memset", dict(ap=mhalf[:, :], constant=-0.5), [], [R_mhalf])
    dmas("sp", [(gpost[:, 0, :], g0_post.partition_broadcast(128)),
                (gpost[:, 1, :], g1_post.partition_broadcast(128))], writes=[R_gpost])

    WCH = 2056

    def make_wstage(stack, tag):
        return (sb(f"wstg_{tag}", [128, 2, WCH], F32, stack), [Res(), Res()])

    def load_weight(dst, src, kc, ncols, g_src, stack, tag, wst):
        CH = WCH
        stg, R_stg = wst
        gt = None
        R_g = Res()
        if g_src is not None:
            gt = sb(f"wg_{tag}", [128, kc], F32, stack)
            dma("sp", gt[:, :], g_src[:, :], writes=[R_g])
        i = 0
        for c in range(kc):
            for c0 in range(0, ncols, CH):
                w = min(CH, ncols - c0)
                b = i % 2
                dma("sp", stg[:, b, 0:w], src[c * 128:(c + 1) * 128, c0:c0 + w], writes=[R_stg[b]])
                if gt is not None:
                    V("dve", "tensor_scalar", dict(
                        out=dst[:, c, c0:c0 + w], in0=stg[:, b, 0:w], scalar1=gt[:, c:c + 1], scalar2=None,
                        op0=ALU.mult), [R_stg[b], R_g], [])
                else:
                    V("dve", "tensor_copy", dict(
                        out=dst[:, c, c0:c0 + w], in_=stg[:, b, 0:w]), [R_stg[b]], [])
                i += 1

    with ExitStack() as st0:
        wst0 = make_wstage(st0, "p")
        load_weight(wout_b[:, 0], w0_out, 8, D, None, st0, "wo0", wst0)
        load_weight(wout_b[:, 1], w1_out, 8, D, None, st0, "wo1", wst0)
        S.barrier()

    def rstd_ops(ss, rstd, n, R_ss, R_rstd):
        V("dve", "tensor_scalar", dict(out=ss, in0=ss, scalar1=1.0 / n, scalar2=EPS,
                                                   op0=ALU.mult, op1=ALU.add), [R_ss], [R_ss])
        V("pool", "tensor_tensor", dict(out=rstd, in0=ss, in1=mhalf[:, :], op=ALU.pow), [R_ss, R_mhalf], [R_rstd])

    def phase_a0(xsrc):
        with ExitStack() as st:
            wbig = sb("a0_wbig", [128, 8, 3232], BF16, st)
            wqb = sb("a0_wqb", [128, 3, 768], BF16, st)
            wkvb = sb("a0_wkvb", [128, 2, 1024], BF16, st)
            wst = make_wstage(st, "a0")
            load_weight(wbig, w0_in, 8, 3232, g0_pre, st, "win0", wst)
            load_weight(wqb, w0_qb, 3, 768, g0_qa, st, "wqb", wst)
            load_weight(wkvb, w0_kvb, 2, 1024, g0_kva, st, "wkvb", wst)
            posi = sb("a0_posi", [128, NB], I32, st)
            posf = sb("a0_posf", [128, NB], F32, st)
            ang = sb("a0_ang", [128, 2, NB, 16], F32, st)
            tmpa = sb("a0_tmpa", [128, 2, NB, 16], F32, st)
            tmpi = sb("a0_tmpi", [128, 2, NB, 16], I32, st)
            trig = sb("a0_trig", [128, 2, NB, 16], F32, st)
            R_pos, R_ang, R_tmp, R_trig = Res(), Res(), Res(), Res()
            dma("sp", posi[:, :], pos_in[:, :], writes=[R_pos])
            V("dve", "tensor_copy", dict(out=posf[:, :], in_=posi[:, :]), [R_pos], [R_pos])
            TWO_PI = 2.0 * np.pi
            for j in range(16):
                invf = float(np.float32(10000.0) ** np.float32(-(2.0 * j) / 32.0))
                V("dve", "tensor_scalar", dict(
                    out=ang[:, 0, :, j], in0=posf[:, :], scalar1=invf, scalar2=None, op0=ALU.mult), [R_pos], [R_ang])
            V("dve", "tensor_scalar", dict(out=ang[:, 1], in0=ang[:, 0], scalar1=float(np.pi / 2),
                                                       scalar2=None, op0=ALU.add), [R_ang], [R_ang])
            V("dve", "tensor_scalar", dict(out=tmpa[:], in0=ang[:], scalar1=float(1.0 / TWO_PI),
                                                       scalar2=None, op0=ALU.mult), [R_ang], [R_tmp])
            V("dve", "tensor_copy", dict(out=tmpi[:], in_=tmpa[:]), [R_tmp], [R_tmp])
            V("dve", "tensor_copy", dict(out=tmpa[:], in_=tmpi[:]), [R_tmp], [R_tmp])
            V("dve", "scalar_tensor_tensor", dict(out=ang[:], in0=tmpa[:], scalar=float(-TWO_PI),
                                                              in1=ang[:], op0=ALU.mult, op1=ALU.add), [R_tmp, R_ang], [R_ang])
            V("dve", "tensor_scalar", dict(out=tmpa[:], in0=ang[:], scalar1=float(np.pi),
                                                       scalar2=float(-TWO_PI), op0=ALU.is_gt, op1=ALU.mult), [R_ang], [R_tmp])
            V("dve", "tensor_tensor", dict(out=ang[:], in0=ang[:], in1=tmpa[:], op=ALU.add), [R_ang, R_tmp], [R_ang])
            V("dve", "tensor_scalar", dict(out=tmpa[:], in0=ang[:], scalar1=float(-np.pi),
                                                       scalar2=float(TWO_PI), op0=ALU.is_lt, op1=ALU.mult), [R_ang], [R_tmp])
            V("dve", "tensor_tensor", dict(out=ang[:], in0=ang[:], in1=tmpa[:], op=ALU.add), [R_ang, R_tmp], [R_ang])
            V("act", "activation", dict(out=trig[:], in_=ang[:], func=AF.Sin), [R_ang], [R_trig])
            sin_t = trig[:, 0]
            cos_t = trig[:, 1]

            xt = sb("a0_xt", [128, 2, D], F32, st)
            junk = sb("a0_junk", [128, 2, D], BF16, st)
            R_jk = [Res(), Res()]
            jkc = [0]
            hb = sb("a0_hb", [128, 2, D], BF16, st)
            hT = sb("a0_hT", [128, 2, 8, 512], BF16, st)
            ssx = sb("a0_ssx", [128, 2, 4], F32, st)
            stg = sb("a0_stg", [128, 4, 512], BF16, st)
            qan = sb("a0_qan", [128, 2, 384], BF16, st)
            ckvn = sb("a0_ckvn", [128, 2, 256], BF16, st)
            krf = sb("a0_krf", [128, 2, 4, 16], F32, st)
            krb = sb("a0_krb", [128, 2, 32], BF16, st)
            qanT = sb("a0_qanT", [128, 2, 3, 512], BF16, st)
            ckvT = sb("a0_ckvT", [128, 2, 2, 512], BF16, st)
            krT = sb("a0_krT", [32, 2, 512], BF16, st)
            qrf = sb("a0_qrf", [128, 2, 4, 8, 16], F32, st)
            qrb = sb("a0_qrb", [128, 2, 256], BF16, st)
            qrT = sb("a0_qrT", [128, 2, 2, 512], BF16, st)
            pT = [ps(f"a0_pT{i}", [128, 1024], BF16, st) for i in range(2)]
            pF = [ps(f"a0_pF{i}", [128, 512], F32, st) for i in range(3)]
            pK = [ps(f"a0_pK{i}", [128, 512], F32, st) for i in range(3)]
            R_xt = [Res(), Res()]
            R_hb = [Res(), Res()]
            R_hT = [Res(), Res()]
            R_ss = [[Res() for _ in range(4)] for _ in range(2)]
            R_rs = [[Res() for _ in range(4)] for _ in range(2)]
            R_junk = Res()
            R_stg = [Res() for _ in range(4)]
            R_sil = [Res(), Res()]
            R_qan, R_ckvn, R_krf, R_krb = [Res(), Res()], [Res(), Res()], [Res(), Res()], [Res(), Res()]
            R_qanT, R_ckvT, R_krT = [Res(), Res()], [Res(), Res()], [Res(), Res()]
            R_qrf, R_qrb, R_qrT = [Res(), Res()], [Res(), Res()], [Res(), Res()]
            R_pT = [Res(), Res()]
            R_pF = [Res() for _ in range(3)]
            R_pK = [Res() for _ in range(3)]
            cnt = {"pT": 0, "pF": 0, "pK": 0, "stg": 0, "sil": 0, "sm": 0}

            def nxt(k, n):
                v = cnt[k] % n
                cnt[k] += 1
                return v

            def store_bf(src_ps, rows, ncols, dsts, R_src, eng="dve"):
                si = nxt("stg", 4)
                if eng == "dve":
                    V("dve", "tensor_copy", dict(out=stg[0:rows, si, 0:ncols], in_=src_ps), [R_src], [R_stg[si]])
                else:
                    V("act", "copy", dict(out=stg[0:rows, si, 0:ncols], in_=src_ps), [R_src], [R_stg[si]])
                dmas("pool", [(d, stg[r0:r1, si, 0:ncols]) for (d, r0, r1) in dsts], reads=[R_stg[si]])

            for T in range(NT):
                hs = T % 2
                for sbk in range(4):
                    tb = T * 4 + sbk
                    xi = tb % 2
                    dma("sp", xt[:, xi, :], xsrc[tb * 128:(tb + 1) * 128, :], writes=[R_xt[xi]])
                    ss = ssx[:, xi, 0:1]
                    rs = ssx[:, xi, 1:2]
                    SQ(junk, R_jk, jkc, ":", dict(in_=xt[:, xi, :],
                                                                         func=AF.Square, accum_out=ss), [R_xt[xi]], [R_ss[xi][0]])
                    rstd_ops(ss, rs, D, R_ss[xi][0], R_rs[xi][0])
                    V("dve", "tensor_scalar", dict(
                        out=hb[:, xi, :], in0=xt[:, xi, :], scalar1=rs, scalar2=None, op0=ALU.mult), [R_xt[xi], R_rs[xi][0]], [R_hb[xi]])
                    pi = nxt("pT", 2)
                    transposes([(pT[pi][:, fc * 128:(fc + 1) * 128], hb[:, xi, fc * 128:(fc + 1) * 128], ident_b)
                                for fc in range(8)], [R_hb[xi], R_cst], [R_pT[pi]])
                    V("dve", "tensor_copy", dict(
                        out=hT[:, hs, :, sbk * 128:(sbk + 1) * 128],
                        in_=pT[pi][:, :].rearrange("p (c t) -> p c t", c=8)), [R_pT[pi]], [R_hT[hs]])
                for sbk in range(4):
                    tb = T * 4 + sbk
                    tsl = slice(sbk * 128, (sbk + 1) * 128)
                    k = nxt("pK", 3)
                    mms([(pK[k][:, 0:512], hT[:, hs, fc, tsl], wbig[:, fc, 2048:2560], fc == 0, fc == 7)
                         for fc in range(8)], [R_hT[hs]], [R_pK[k]])
                    store_bf(pK[k][:, 0:512], 128, 512, [(SV[tb * 128:(tb + 1) * 128, :], 0, 128)], R_pK[k], "act")
                    k = nxt("pK", 3)
                    mms([(pK[k][:, 0:384], hT[:, hs, fc, tsl], wbig[:, fc, 2560:2944], fc == 0, fc == 7)
                         for fc in range(8)], [R_hT[hs]], [R_pK[k]])
                    sm = nxt("sm", 2)
                    ss = ssx[:, sm, 2:3]
                    rs = ssx[:, sm, 3:4]
                    SQ(junk, R_jk, jkc, "0:384", dict(in_=pK[k][:, 0:384],
                                                                       func=AF.Square, accum_out=ss), [R_pK[k]], [R_ss[sm][1]])
                    rstd_ops(ss, rs, 384, R_ss[sm][1], R_rs[sm][1])
                    V("dve", "tensor_scalar", dict(
                        out=qan[:, sm, :], in0=pK[k][:, 0:384], scalar1=rs, scalar2=None, op0=ALU.mult), [R_pK[k], R_rs[sm][1]], [R_qan[sm]])
                    pi = nxt("pT", 2)
                    transposes([(pT[pi][:, j * 128:(j + 1) * 128], qan[:, sm, j * 128:(j + 1) * 128], ident_b)
                                for j in range(3)], [R_qan[sm], R_cst], [R_pT[pi]])
                    V("dve", "tensor_copy", dict(
                        out=qanT[:, hs, :, tsl], in_=pT[pi][:, 0:384].rearrange("p (c t) -> p c t", c=3)), [R_pT[pi]], [R_qanT[hs]])
                    k = nxt("pK", 3)
                    mms([(pK[k][:, 0:288], hT[:, hs, fc, tsl], wbig[:, fc, 2944:3232], fc == 0, fc == 7)
                         for fc in range(8)], [R_hT[hs]], [R_pK[k]])
                    ssc = kv_stats[:, sm, 0:1]
                    rsc = kv_stats[:, sm, 1:2]
                    SQ(junk, R_jk, jkc, "0:256", dict(in_=pK[k][:, 0:256],
                                                                         func=AF.Square, accum_out=ssc), [R_pK[k]], [R_kvs[sm][0]])
                    rstd_ops(ssc, rsc, 256, R_kvs[sm][0], R_kvs[sm][1])
                    V("dve", "tensor_scalar", dict(
                        out=ckvn[:, sm, :], in0=pK[k][:, 0:256], scalar1=rsc, scalar2=None, op0=ALU.mult), [R_pK[k], R_kvs[sm][1]], [R_ckvn[sm]])
                    cs = cos_t[:, tb, :]
                    sn = sin_t[:, tb, :]
                    x1 = pK[k][:, 256:272]
                    x2 = pK[k][:, 272:288]
                    V("dve", "tensor_tensor", dict(
                        out=krf[:, sm, 0, :], in0=x1, in1=cs, op=ALU.mult), [R_pK[k], R_trig], [R_krf[sm]])
                    V("dve", "tensor_tensor", dict(
                        out=krf[:, sm, 1, :], in0=x2, in1=sn, op=ALU.mult), [R_pK[k], R_trig], [R_krf[sm]])
                    V("dve", "tensor_tensor", dict(
                        out=krf[:, sm, 2, :], in0=x2, in1=cs, op=ALU.mult), [R_pK[k], R_trig], [R_krf[sm]])
                    V("dve", "tensor_tensor", dict(
                        out=krf[:, sm, 3, :], in0=x1, in1=sn, op=ALU.mult), [R_pK[k], R_trig], [R_krf[sm]])
                    V("dve", "tensor_tensor", dict(
                        out=krb[:, sm, 0:16], in0=krf[:, sm, 0, :], in1=krf[:, sm, 1, :], op=ALU.subtract), [R_krf[sm]], [R_krb[sm]])
                    V("dve", "tensor_tensor", dict(
                        out=krb[:, sm, 16:32], in0=krf[:, sm, 2, :], in1=krf[:, sm, 3, :], op=ALU.add), [R_krf[sm]], [R_krb[sm]])
                    pi = nxt("pT", 2)
                    transposes([(pT[pi][:, j * 128:(j + 1) * 128], ckvn[:, sm, j * 128:(j + 1) * 128], ident_b)
                                for j in range(2)] +
                               [(pT[pi][0:32, 256:384], krb[:, sm, :], ident_b)],
                               [R_ckvn[sm], R_krb[sm], R_cst], [R_pT[pi]])
                    V("dve", "tensor_copy", dict(
                        out=ckvT[:, hs, :, tsl], in_=pT[pi][:, 0:256].rearrange("p (c t) -> p c t", c=2)), [R_pT[pi]], [R_ckvT[hs]])
                    V("dve", "tensor_copy", dict(
                        out=krT[:, hs, tsl], in_=pT[pi][0:32, 256:384]), [R_pT[pi]], [R_krT[hs]])
                for sbk in range(4):
                    tb = T * 4 + sbk
                    tsl = slice(sbk * 128, (sbk + 1) * 128)
                    k = nxt("pK", 3)
                    mms([(pK[k][:, 0:256], qanT[:, hs, j, tsl], wqb[:, j, 512:768], j == 0, j == 2)
                         for j in range(3)], [R_qanT[hs]], [R_pK[k]])
                    sm = nxt("sm", 2)
                    pv = pK[k][:, 0:256].rearrange("p (h r) -> p h r", h=8)
                    x1 = pv[:, :, 0:16]
                    x2 = pv[:, :, 16:32]
                    cs = cos_t[:, tb, :].unsqueeze(1).broadcast_to([128, 8, 16])
                    sn = sin_t[:, tb, :].unsqueeze(1).broadcast_to([128, 8, 16])
                    for idx, (a, b_) in enumerate([(x1, cs), (x2, sn), (x2, cs), (x1, sn)]):
                        V("dve", "tensor_tensor", dict(
                            out=qrf[:, sm, idx], in0=a, in1=b_, op=ALU.mult), [R_pK[k], R_trig], [R_qrf[sm]])
                    qv = qrb[:, sm, :].rearrange("p (h r) -> p h r", h=8)
                    V("dve", "tensor_tensor", dict(
                        out=qv[:, :, 0:16], in0=qrf[:, sm, 0], in1=qrf[:, sm, 1], op=ALU.subtract), [R_qrf[sm]], [R_qrb[sm]])
                    V("dve", "tensor_tensor", dict(
                        out=qv[:, :, 16:32], in0=qrf[:, sm, 2], in1=qrf[:, sm, 3], op=ALU.add), [R_qrf[sm]], [R_qrb[sm]])
                    pi = nxt("pT", 2)
                    transposes([(pT[pi][:, j * 128:(j + 1) * 128], qrb[:, sm, j * 128:(j + 1) * 128], ident_b)
                                for j in range(2)], [R_qrb[sm], R_cst], [R_pT[pi]])
                    V("dve", "tensor_copy", dict(
                        out=qrT[:, hs, :, tsl], in_=pT[pi][:, 0:256].rearrange("p (c t) -> p c t", c=2)), [R_pT[pi]], [R_qrT[hs]])
                    k = nxt("pK", 3)
                    mms([(pK[k][:, 0:512], ckvT[:, hs, j, tsl], wkvb[:, j, 512:1024], j == 0, j == 1)
                         for j in range(2)], [R_ckvT[hs]], [R_pK[k]])
                    store_bf(pK[k][:, 0:512], 128, 512, [(MV[tb * 128:(tb + 1) * 128, :], 0, 128)], R_pK[k], "act")
                tcs = slice(T * 512, (T + 1) * 512)
                for c in range(16):
                    f = nxt("pF", 3)
                    mms([(pF[f][:, :], wbig[:, fc, c * 128:(c + 1) * 128], hT[:, hs, fc, :], fc == 0, fc == 7)
                         for fc in range(8)], [R_hT[hs]], [R_pF[f]])
                    if c < 8:
                        store_bf(pF[f][:, :], 128, 512, [(SQK[c][:, tcs], 0, 128)], R_pF[f])
                    else:
                        si = nxt("stg", 4)
                        V("act", "activation", dict(out=stg[:, si, :], in_=pF[f][:, :],
                                                                           func=AF.Silu), [R_pF[f]], [R_stg[si]])
                        dma("pool", SG[c - 8][:, tcs], stg[:, si, :], reads=[R_stg[si]])
                for c in range(4):
                    f = nxt("pF", 3)
                    mms([(pF[f][:, :], wqb[:, j, c * 128:(c + 1) * 128], qanT[:, hs, j, :], j == 0, j == 2)
                         for j in range(3)], [R_qanT[hs]], [R_pF[f]])
                    store_bf(pF[f][:, :], 128, 512, [(MQ[2 * c][0:64, tcs], 0, 64), (MQ[2 * c + 1][0:64, tcs], 64, 128)],
                             R_pF[f])
                    f = nxt("pF", 3)
                    mms([(pF[f][:, :], wkvb[:, j, c * 128:(c + 1) * 128], ckvT[:, hs, j, :], j == 0, j == 1)
                         for j in range(2)], [R_ckvT[hs]], [R_pF[f]])
                    store_bf(pF[f][:, :], 128, 512, [(MK[2 * c][:, tcs], 0, 64), (MK[2 * c + 1][:, tcs], 64, 128)],
                             R_pF[f])
                dmas("pool", [(MQ[4 * j + i][64:96, tcs], qrT[i * 32:(i + 1) * 32, hs, j, :])
                              for j in range(2) for i in range(4)], reads=[R_qrT[hs]])
                dma("pool", MKR[:, tcs], krT[:, hs, :], reads=[R_krT[hs]])
            S.barrier()

    kv_stats = sb("kv_stats", [128, 2, 2], F32)
    R_kvs = [[Res(), Res()], [Res(), Res()]]

    def phase_attn(heads, kind_tag):
        with ExitStack() as st:
            QT = sb(f"{kind_tag}_QT", [128, 2, S_LEN], BF16, st)
            KT = sb(f"{kind_tag}_KT", [128, 2, S_LEN], BF16, st)
            VT = sb(f"{kind_tag}_VT", [128, 2, NB, 128], BF16, st)
            SGt = sb(f"{kind_tag}_SG", [128, 2, QW], BF16, st)
            PT = sb(f"{kind_tag}_PT", [128, 3, QW], BF16, st)
            Ef = sb(f"{kind_tag}_Ef", [128, 2, QW], F32, st)
            SPt = sb(f"{kind_tag}_SP", [128, 2, QW], BF16, st)
            Ssum = sb(f"{kind_tag}_Ss", [128, 2, QW], BF16, st)
            rl = sb(f"{kind_tag}_rl", [128, 2, QW], F32, st)
            t1 = sb(f"{kind_tag}_t1", [128, 2, QW], F32, st)
            ogt = sb(f"{kind_tag}_og", [128, 2, QW], BF16, st)
            any_sb = any(h["kind"] == "sb" for h in heads)
            nZ = 3 if any_sb else 2
            nO = 1 if any_sb else 2
            Z = [ps(f"{kind_tag}_Z{i}", [128, QW], F32, st) for i in range(nZ)]
            O = [ps(f"{kind_tag}_O{i}", [128, QW], F32, st) for i in range(nO)]
            R_QT, R_KT, R_VT = [Res(), Res()], [Res(), Res()], [Res(), Res()]
            R_SG = [Res(), Res()]
            R_PT = [Res() for _ in range(3)]
            R_Ef, R_SP, R_Ss = [Res(), Res()], [Res(), Res()], [Res(), Res()]
            R_rl, R_t1, R_og = [Res(), Res()], [Res(), Res()], [Res(), Res()]
            R_Z = [Res() for _ in range(nZ)]
            R_O = [Res() for _ in range(nO)]
            V("pool", "memset", dict(ap=VT[:, 0, :, 64:128], constant=1.0), [], [R_VT[0]])
            V("pool", "memset", dict(ap=VT[:, 1, :, 0:64], constant=1.0), [], [R_VT[1]])
            if any(h["kind"] == "fox" for h in heads):
                V("pool", "memset", dict(ap=KT[64:65, 0, :], constant=1.0), [], [R_KT[0]])
                V("pool", "memset", dict(ap=KT[64:65, 1, :], constant=1.0), [], [R_KT[1]])

            def load_head(hi):
                h = heads[hi]
                b = hi % 2
                par = h["par"]
                dmas("sp", [(QT[r0:r1, b, :], src) for (src, r0, r1) in h["q"]], writes=[R_QT[b]])
                dmas("sp", [(KT[r0:r1, b, :], src) for (src, r0, r1) in h["k"]], writes=[R_KT[b]])
                vsrc = h["v"]
                vc = slice(0, 64) if par == 0 else slice(64, 128)
                vv = vsrc.rearrange("(blk p) c -> p blk c", p=128)
                dmas("sp", [(VT[:, b, q4 * 8:(q4 + 1) * 8, vc], vv[:, q4 * 8:(q4 + 1) * 8, :]) for q4 in range(NB // 8)],
                     writes=[R_VT[b]])

            zc = [0]
            oc = [0]
            sgc = [0]
            ptc = [0]

            def compute_head(hi):
                h = heads[hi]
                b = hi % 2
                kind = h["kind"]
                par = h["par"]
                kd = h["kd"]
                scale = h["scale"]
                rowsO = slice(0, 64) if par == 0 else slice(64, 128)
                rowsL = slice(64, 128) if par == 0 else slice(0, 64)
                mask = mlt_b if kind == "sb" else mle_b
                for t in range(S_LEN // QW):
                    q0 = t * QW
                    nq = QW // 128
                    sgi = sgc[0] % 2
                    sgc[0] += 1
                    dma("sp", SGt[rowsO, sgi, :], h["g"][rowsO, q0:q0 + QW], writes=[R_SG[sgi]])
                    oi = oc[0] % nO
                    oc[0] += 1
                    pairs = []
                    for kb in range(nq * t + nq - 1, -1, -1):
                        c0 = max(0, kb - nq * t) * 128
                        pairs.append((kb, c0, kb >= nq * t))
                    first_bank = [True, True]
                    if kind == "sb":
                        V("pool", "memset", dict(ap=Ssum[:, 0, :], constant=0.0), [], [R_Ss[0]])
                        V("pool", "memset", dict(ap=Ssum[:, 1, :], constant=0.0), [], [R_Ss[1]])
                    zis = []

                    def banks(c0):
                        out = []
                        if c0 < 512:
                            out.append((c0, 512))
                        out.append((max(c0, 512), QW))
                        return out

                    def emit_qk(i):
                        kb, c0, diag = pairs[i]
                        zi = zc[0] % nZ
                        zc[0] += 1
                        zis.append(zi)
                        mms([(Z[zi][:, lo:hi], KT[0:kd, b, kb * 128:(kb + 1) * 128], QT[0:kd, b, q0 + lo:q0 + hi],
                              True, True) for (lo, hi) in banks(c0)],
                            [R_KT[b], R_QT[b]], [R_Z[zi]])

                    pts = {}

                    def emit_p(i):
                        kb, c0, diag = pairs[i]
                        zi = zis[i]
                        if kind != "sb":
                            pi = ptc[0] % 3
                            ptc[0] += 1
                            pts[i] = pi
                            if kind == "fox":
                                bias = negc[:, kb, h["hidx"]:h["hidx"] + 1]
                                V("act", "activation", dict(out=PT[:, pi, c0:QW], in_=Z[zi][:, c0:QW],
                                                                        func=AF.Exp, bias=bias, scale=scale), [R_Z[zi], R_negc], [R_PT[pi]])
                            else:
                                V("act", "activation", dict(out=PT[:, pi, c0:QW], in_=Z[zi][:, c0:QW],
                                                                        func=AF.Exp, scale=scale), [R_Z[zi]], [R_PT[pi]])
                            if diag:
                                V("dve", "tensor_tensor", dict(out=PT[:, pi, c0:c0 + 128],
                                                                           in0=PT[:, pi, c0:c0 + 128], in1=mask,
                                                                           op=ALU.mult), [R_PT[pi], R_cst], [R_PT[pi]])
                        else:
                            ei = i % 2
                            V("act", "activation", dict(out=Ef[:, ei, c0:QW], in_=Z[zi][:, c0:QW],
                                                                    func=AF.Exp, scale=scale), [R_Z[zi]], [R_Ef[ei]])
                            V("act", "activation", dict(out=SPt[:, ei, c0:QW], in_=Ef[:, ei, c0:QW],
                                                                    func=AF.Ln, bias=1.0, scale=1.0), [R_Ef[ei]], [R_SP[ei]])
                            if diag:
                                V("dve", "tensor_tensor", dict(out=SPt[:, ei, c0:c0 + 128],
                                                                           in0=SPt[:, ei, c0:c0 + 128], in1=mask,
                                                                           op=ALU.mult), [R_SP[ei], R_cst], [R_SP[ei]])

                    def emit_tri(i):
                        kb, c0, diag = pairs[i]
                        zi = zis[i]
                        ei = i % 2
                        si = i % 2
                        specs = []
                        for (lo, hi) in banks(c0):
                            specs.append((Z[zi][:, lo:hi], tri_b, SPt[:, ei, lo:hi], False, i == 0, "skip"))
                            if i > 0:
                                specs.append((Z[zi][:, lo:hi], onesn_b, Ssum[:, si, lo:hi], False, True, "skip"))
                        mms(specs, [R_SP[ei], R_Ss[si], R_cst], [R_Z[zi]])
                        if i + 1 < len(pairs):
                            V("dve", "tensor_tensor", dict(out=Ssum[:, 1 - si, c0:QW],
                                                                       in0=Ssum[:, si, c0:QW], in1=SPt[:, ei, c0:QW],
                                                                       op=ALU.add), [R_Ss[si], R_SP[ei]], [R_Ss[1 - si]])

                    def emit_w(i):
                        kb, c0, diag = pairs[i]
                        zi = zis[i]
                        pi = ptc[0] % 3
                        ptc[0] += 1
                        pts[i] = pi
                        V("act", "activation", dict(out=PT[:, pi, c0:QW], in_=Z[zi][:, c0:QW],
                                                                func=AF.Exp, scale=scale), [R_Z[zi]], [R_PT[pi]])
                        if diag:
                            V("dve", "tensor_tensor", dict(out=PT[:, pi, c0:c0 + 128],
                                                                       in0=PT[:, pi, c0:c0 + 128], in1=mask,
                                                                       op=ALU.mult), [R_PT[pi], R_cst], [R_PT[pi]])

                    def emit_pv(i):
                        kb, c0, diag = pairs[i]
                        pi = pts[i]
                        specs = []
                        last = (i == len(pairs) - 1)
                        for (lo, hi) in banks(c0):
                            bk = 0 if lo < 512 else 1
                            if kind == "sb":
                                lhsT = VT[:, b, kb, rowsO]
                                out = O[oi][rowsO, lo:hi]
                            else:
                                lhsT = VT[:, b, kb, :]
                                out = O[oi][:, lo:hi]
                            specs.append((out, lhsT, PT[:, pi, lo:hi], first_bank[bk], last, "skip"))
                            first_bank[bk] = False
                        mms(specs, [R_PT[pi], R_VT[b]], [R_O[oi]])

                    n = len(pairs)
                    if kind != "sb":
                        emit_qk(0)
                        for i in range(n):
                            if i + 1 < n:
                                emit_qk(i + 1)
                            emit_p(i)
                            emit_pv(i)
                    else:
                        emit_qk(0)
                        emit_p(0)
                        for i in range(n):
                            if i + 1 < n:
                                emit_qk(i + 1)
                                emit_p(i + 1)
                            emit_tri(i)
                            emit_w(i)
                            emit_pv(i)
                    ri = t % 2
                    if kind == "sb":
                        V("dve", "tensor_tensor", dict(out=ogt[rowsO, ri, :], in0=O[oi][rowsO, :],
                                                                   in1=SGt[rowsO, sgi, :], op=ALU.mult), [R_O[oi], R_SG[sgi]], [R_og[ri]])
                    else:
                        V("dve", "reciprocal", dict(out=rl[rowsO, ri, :], in_=O[oi][rowsL, :]), [R_O[oi]], [R_rl[ri]])
                        V("dve", "tensor_tensor", dict(out=t1[rowsO, ri, :], in0=O[oi][rowsO, :],
                                                                   in1=rl[rowsO, ri, :], op=ALU.mult), [R_O[oi], R_rl[ri]], [R_t1[ri]])
                        V("pool", "tensor_tensor", dict(out=ogt[rowsO, ri, :], in0=t1[rowsO, ri, :],
                                                                    in1=SGt[rowsO, sgi, :], op=ALU.mult), [R_t1[ri], R_SG[sgi]], [R_og[ri]])
                    dma("pool", h["o"][rowsO, q0:q0 + QW], ogt[rowsO, ri, :], reads=[R_og[ri]])

            load_head(0)
            for hi in range(len(heads)):
                if hi + 1 < len(heads):
                    load_head(hi + 1)
                compute_head(hi)
            S.barrier()

    def phase_c(layer, xsrc, dst):
        with ExitStack() as st:
            og = sb(f"c{layer}_og", [128, 2, 8, 512], BF16, st)
            xt = sb(f"c{layer}_xt", [128, 2, D], F32, st)
            yt = sb(f"c{layer}_yt", [128, 2, D], F32, st)
            junk = sb(f"c{layer}_junk", [128, 2, D], BF16, st)
            R_jk = [Res(), Res()]
            jkc = [0]
            stt = sb(f"c{layer}_st", [128, 2, 2], F32, st)
            Y = [ps(f"c{layer}_Y{i}", [128, D], F32, st) for i in range(3)]
            R_ogc, R_xt, R_yt = [Res(), Res()], [Res(), Res()], [Res(), Res()]
            R_junk = Res()
            R_ss, R_rs = [Res(), Res()], [Res(), Res()]
            R_Y = [Res() for _ in range(3)]
            for T in range(NT):
                oi = T % 2
                dma("sp", og[:, oi], OGT[:, :, T * 512:(T + 1) * 512].rearrange("c p t -> p c t"),
                    writes=[R_ogc[oi]])
                for sbk in range(4):
                    tb = T * 4 + sbk
                    xi = tb % 2
                    yi = tb % 3
                    dma("sp", xt[:, xi, :], xsrc[tb * 128:(tb + 1) * 128, :], writes=[R_xt[xi]])
                    mms([(Y[yi][:, hf * 512:(hf + 1) * 512], og[:, oi, fc, sbk * 128:(sbk + 1) * 128],
                          wout_b[:, layer, fc, hf * 512:(hf + 1) * 512], fc == 0, fc == 7)
                         for hf in range(2) for fc in range(8)], [R_ogc[oi], R_wout], [R_Y[yi]])
                    ss = stt[:, xi, 0:1]
                    rs = stt[:, xi, 1:2]
                    SQ(junk, R_jk, jkc, ":", dict(in_=Y[yi][:, :],
                                                                         func=AF.Square, accum_out=ss), [R_Y[yi]], [R_ss[xi]])
                    rstd_ops(ss, rs, D, R_ss[xi], R_rs[xi])
                    V("dve", "scalar_tensor_tensor", dict(
                        out=yt[:, xi, :], in0=Y[yi][:, :], scalar=rs, in1=gpost[:, layer, :],
                        op0=ALU.mult, op1=ALU.mult), [R_Y[yi], R_rs[xi], R_gpost], [R_yt[xi]])
                    V("pool", "tensor_tensor", dict(out=yt[:, xi, :], in0=yt[:, xi, :],
                                                                       in1=xt[:, xi, :], op=ALU.add), [R_yt[xi], R_xt[xi]], [R_yt[xi]])
                    dma("pool", dst[tb * 128:(tb + 1) * 128, :], yt[:, xi, :], reads=[R_yt[xi]])
            S.barrier()

    def phase_a1(xsrc):
        with ExitStack() as st:
            wbig = sb("a1_wbig", [128, 8, 4112], BF16, st)
            wst = make_wstage(st, "a1")
            load_weight(wbig, w1_in, 8, 4112, g1_pre, st, "win1", wst)
            xt = sb("a1_xt", [128, 2, D], F32, st)
            junk = sb("a1_junk", [128, 2, D], BF16, st)
            R_jk = [Res(), Res()]
            jkc = [0]
            hb = sb("a1_hb", [128, 2, D], BF16, st)
            hT = sb("a1_hT", [128, 2, 8, 512], BF16, st)
            ssx = sb("a1_ssx", [128, 2, 2], F32, st)
            stg = sb("a1_stg", [128, 4, 512], BF16, st)
            fl = sb("a1_fl", [128, NB, 16], F32, st)
            fl2 = sb("a1_fl2", [128, NB, 16], F32, st)
            bfb = sb("a1_bfb", [128, 16], F32, st)
            carry = sb("a1_carry", [128, NB, 16], F32, st)
            mrow = sb("a1_mrow", [16, S_LEN], BF16, st)
            pT = [ps(f"a1_pT{i}", [128, 1024], BF16, st) for i in range(2)]
            pF = [ps(f"a1_pF{i}", [128, 512], F32, st) for i in range(3)]
            pK = [ps(f"a1_pK{i}", [128, 512], F32, st) for i in range(3)]
            R_xt, R_hb, R_hT = [Res(), Res()], [Res(), Res()], [Res(), Res()]
            R_ss, R_rs = [Res(), Res()], [Res(), Res()]
            R_junk = Res()
            R_stg = [Res() for _ in range(4)]
            R_pT = [Res(), Res()]
            R_pF = [Res() for _ in range(3)]
            R_pK = [Res() for _ in range(3)]
            R_fl, R_fl2, R_bfb, R_carry, R_mrow = Res(), Res(), Res(), Res(), Res()
            cnt = {"pT": 0, "pF": 0, "pK": 0, "stg": 0}

            def nxt(k, n):
                v = cnt[k] % n
                cnt[k] += 1
                return v

            def store_bf(src_ps, rows, ncols, dsts, R_src, eng="dve"):
                si = nxt("stg", 4)
                if eng == "dve":
                    V("dve", "tensor_copy", dict(out=stg[0:rows, si, 0:ncols], in_=src_ps), [R_src], [R_stg[si]])
                else:
                    V("act", "copy", dict(out=stg[0:rows, si, 0:ncols], in_=src_ps), [R_src], [R_stg[si]])
                dmas("pool", [(d, stg[r0:r1, si, 0:ncols]) for (d, r0, r1) in dsts], reads=[R_stg[si]])

            dma("sp", bfb[:, :], b1_f.partition_broadcast(128), writes=[R_bfb])
            for T in range(NT):
                hs = T % 2
                tcs = slice(T * 512, (T + 1) * 512)
                for sbk in range(4):
                    tb = T * 4 + sbk
                    xi = tb % 2
                    dma("sp", xt[:, xi, :], xsrc[tb * 128:(tb + 1) * 128, :], writes=[R_xt[xi]])
                    ss = ssx[:, xi, 0:1]
                    rs = ssx[:, xi, 1:2]
                    SQ(junk, R_jk, jkc, ":", dict(in_=xt[:, xi, :],
                                                                         func=AF.Square, accum_out=ss), [R_xt[xi]], [R_ss[xi]])
                    rstd_ops(ss, rs, D, R_ss[xi], R_rs[xi])
                    V("dve", "tensor_scalar", dict(
                        out=hb[:, xi, :], in0=xt[:, xi, :], scalar1=rs, scalar2=None, op0=ALU.mult), [R_xt[xi], R_rs[xi]], [R_hb[xi]])
                    pi = nxt("pT", 2)
                    transposes([(pT[pi][:, fc * 128:(fc + 1) * 128], hb[:, xi, fc * 128:(fc + 1) * 128], ident_b)
                                for fc in range(8)], [R_hb[xi], R_cst], [R_pT[pi]])
                    V("dve", "tensor_copy", dict(
                        out=hT[:, hs, :, sbk * 128:(sbk + 1) * 128],
                        in_=pT[pi][:, :].rearrange("p (c t) -> p c t", c=8)), [R_pT[pi]], [R_hT[hs]])
                for sbk in range(4):
                    tb = T * 4 + sbk
                    tsl = slice(sbk * 128, (sbk + 1) * 128)
                    for vh in range(2):
                        k = nxt("pK", 3)
                        mms([(pK[k][:, 0:512], hT[:, hs, fc, tsl], wbig[:, fc, 3072 + vh * 512:3072 + (vh + 1) * 512],
                              fc == 0, fc == 7) for fc in range(8)], [R_hT[hs]], [R_pK[k]])
                        store_bf(pK[k][:, 0:512], 128, 512,
                                 [(FV[tb * 128:(tb + 1) * 128, vh * 512:(vh + 1) * 512], 0, 128)], R_pK[k],
                                 "act" if vh == 0 else "dve")
                    k = nxt("pK", 3)
                    mms([(pK[k][:, 0:16], hT[:, hs, fc, tsl], wbig[:, fc, 4096:4112], fc == 0, fc == 7)
                         for fc in range(8)], [R_hT[hs]], [R_pK[k]])
                    V("dve", "tensor_tensor", dict(out=fl[:, tb, :], in0=pK[k][:, 0:16],
                                                                           in1=bfb[:, :], op=ALU.add), [R_pK[k], R_bfb], [R_fl])
                for c in range(24):
                    f = nxt("pF", 3)
                    mms([(pF[f][:, :], wbig[:, fc, c * 128:(c + 1) * 128], hT[:, hs, fc, :], fc == 0, fc == 7)
                         for fc in range(8)], [R_hT[hs]], [R_pF[f]])
                    if c < 8:
                        store_bf(pF[f][:, :], 128, 512, [(FQ[2 * c][0:64, tcs], 0, 64), (FQ[2 * c + 1][0:64, tcs], 64, 128)],
                                 R_pF[f], "dve" if c % 2 else "act")
                    elif c < 16:
                        cc = c - 8
                        store_bf(pF[f][:, :], 128, 512, [(FK[2 * cc][:, tcs], 0, 64), (FK[2 * cc + 1][:, tcs], 64, 128)],
                                 R_pF[f], "dve" if c % 2 else "act")
                    else:
                        si = nxt("stg", 4)
                        V("act", "activation", dict(out=stg[:, si, :], in_=pF[f][:, :],
                                                                           func=AF.Silu), [R_pF[f]], [R_stg[si]])
                        dma("pool", FG[c - 16][:, tcs], stg[:, si, :], reads=[R_stg[si]])
            flv = fl[:, :, :].rearrange("p b h -> p (b h)")
            fl2v = fl2[:, :, :].rearrange("p b h -> p (b h)")
            V("act", "activation", dict(out=fl2v, in_=flv, func=AF.Exp, scale=-1.0), [R_fl], [R_fl2])
            V("act", "activation", dict(out=flv, in_=fl2v, func=AF.Ln, bias=1.0, scale=1.0), [R_fl2], [R_fl])
            k = nxt("pK", 3)
            k2 = nxt("pK", 3)
            mms([(pK[k][:, 0:NB * 16], triC_f, flv, True, True)], [R_fl, R_cst], [R_pK[k]])
            mms([(pK[k2][:, 0:NB * 16], ones_f, flv, True, True)], [R_fl, R_cst], [R_pK[k2]])
            tot = fl2
            V("dve", "tensor_copy", dict(out=fl2v, in_=pK[k2][:, 0:NB * 16]), [R_pK[k2]], [R_fl2])
            V("pool", "memset", dict(ap=carry[:, 0, :], constant=0.0), [], [R_carry])
            for bk in range(1, NB):
                V("dve", "tensor_tensor", dict(out=carry[:, bk, :], in0=carry[:, bk - 1, :],
                                                                  in1=tot[:, bk - 1, :], op=ALU.add), [R_carry, R_fl2], [R_carry])
            V("dve", "tensor_tensor", dict(out=negc[:, :, :].rearrange("p b h -> p (b h)"),
                                                       in0=pK[k][:, 0:NB * 16],
                                                       in1=carry[:, :, :].rearrange("p b h -> p (b h)"), op=ALU.add), [R_pK[k], R_carry], [R_negc])
            for g4 in range(NT):
                f = nxt("pF", 3)
                transposes([(pF[f][0:16, j * 128:(j + 1) * 128], negc[:, g4 * 4 + j, :], ident_f) for j in range(4)],
                           [R_negc, R_cst], [R_pF[f]])
                V("dve", "tensor_scalar", dict(
                    out=mrow[:, g4 * 512:(g4 + 1) * 512], in0=pF[f][0:16, :], scalar1=-8.0, scalar2=None,
                    op0=ALU.mult), [R_pF[f]], [R_mrow])
            dma("pool", FQ[:, 64, :], mrow[:, :], reads=[R_mrow])
            S.barrier()

    if 0 in layers:
        phase_a0(x_in)
        heads0 = []
        for hh in range(8):
            heads0.append(dict(kind="sb", par=hh % 2, kd=64, scale=0.125,
                               q=[(SQK[hh // 2][(hh % 2) * 64:(hh % 2) * 64 + 64, :], 0, 64)],
                               k=[(SQK[4 + hh // 2][(hh % 2) * 64:(hh % 2) * 64 + 64, :], 0, 64)],
                               v=SV[:, hh * 64:(hh + 1) * 64], g=SG[hh // 2], o=OGT[hh // 2]))
        for hh in range(8):
            heads0.append(dict(kind="mla", par=hh % 2, kd=96, scale=float(96 ** -0.5),
                               q=[(MQ[hh], 0, 96)], k=[(MK[hh], 0, 64), (MKR, 64, 96)],
                               v=MV[:, hh * 64:(hh + 1) * 64], g=SG[4 + hh // 2], o=OGT[4 + hh // 2]))
        phase_attn(heads0, "b0")
        phase_c(0, x_in, X1 if 1 in layers else y_out)
    if 1 in layers:
        xs1 = X1 if 0 in layers else x_in
        phase_a1(xs1)
        heads1 = []
        for hh in range(16):
            heads1.append(dict(kind="fox", par=hh % 2, kd=65, scale=0.125, hidx=hh,
                               q=[(FQ[hh], 0, 65)], k=[(FK[hh], 0, 64)],
                               v=FV[:, hh * 64:(hh + 1) * 64], g=FG[hh // 2], o=OGT[hh // 2]))
        phase_attn(heads1, "b1")
        phase_c(1, xs1, y_out)
    S.emit()
    es.close()
    return nc


def make_consts():
    c = np.zeros((128, 1024), np.float32)
    i = np.arange(128)
    c[:, 0:128] = np.eye(128, dtype=np.float32)
    c[:, 128:256] = np.where(i[:, None] >= i[None, :], -8.0, 0.0)
    c[:, 256:384] = -8.0
    c[:, 384:512] = (i[:, None] <= i[None, :]).astype(np.float32)
    c[:, 512:640] = (i[:, None] < i[None, :]).astype(np.float32)
    c[:, 640:768] = (i[:, None] <= i[None, :]).astype(np.float32)
    c[:, 768:896] = 1.0
    return c


def perm_w0_in(w):
    return np.ascontiguousarray(np.concatenate(
        [w[:, 0:512], w[:, 512:1024], w[:, 1536:2048], w[:, 2720:3232], w[:, 1024:1536], w[:, 2048:2432],
         w[:, 2432:2720]], axis=1))


def perm_w_qb(w):
    w3 = w.reshape(384, 8, 96)
    return np.ascontiguousarray(np.concatenate([w3[:, :, 0:64].reshape(384, 512), w3[:, :, 64:96].reshape(384, 256)],
                                               axis=1))


def perm_w_kvb(w):
    w3 = w.reshape(256, 8, 128)
    return np.ascontiguousarray(np.concatenate([w3[:, :, 0:64].reshape(256, 512), w3[:, :, 64:128].reshape(256, 512)],
                                               axis=1))


def perm_w1_in(w):
    return np.ascontiguousarray(np.concatenate([w[:, 0:2048], w[:, 3072:4096], w[:, 2048:3072], w[:, 4096:4112]],
                                               axis=1))


_NC_CACHE = {}


def make_in_maps(inputs):
    f32 = lambda a: np.ascontiguousarray(np.asarray(a, dtype=np.float32))
    pcol = lambda a: np.ascontiguousarray(np.asarray(a, dtype=np.float32).reshape(-1, 128).T)
    shared = {
        "consts": make_consts(),
        "l0_pre_g": pcol(inputs["l0_pre_g"]), "l0_post_g": f32(inputs["l0_post_g"]),
        "l0_w_in": perm_w0_in(f32(inputs["l0_w_in"])),
        "l0_q_a_g": pcol(inputs["l0_q_a_g"]), "l0_w_q_b": perm_w_qb(f32(inputs["l0_w_q_b"])),
        "l0_kv_a_g": pcol(inputs["l0_kv_a_g"]), "l0_w_kv_b": perm_w_kvb(f32(inputs["l0_w_kv_b"])),
        "l0_w_out": f32(inputs["l0_w_out"]),
        "l1_pre_g": pcol(inputs["l1_pre_g"]), "l1_post_g": f32(inputs["l1_post_g"]),
        "l1_w_in": perm_w1_in(f32(inputs["l1_w_in"])), "l1_b_f": f32(inputs["l1_b_f"]),
        "l1_w_out": f32(inputs["l1_w_out"]),
    }
    x = f32(inputs["x"])
    pos = np.ascontiguousarray(np.asarray(inputs["positions"], dtype=np.int32))
    maps = []
    for b in range(8):
        m = dict(shared)
        m["x"] = np.ascontiguousarray(x[b])
        m["pos"] = np.ascontiguousarray(pos[b].reshape(NB, 128).T)
        maps.append(m)
    return maps


def kernel(**inputs):
    if "nc" not in _NC_CACHE:
        _NC_CACHE["nc"] = build_nc()
    nc = _NC_CACHE["nc"]
    in_maps = make_in_maps(inputs)
    res = run_bass_kernel_spmd(nc, in_maps, core_ids=list(range(8)))
    out = np.stack([np.asarray(r["y"], dtype=np.float32) for r in res.results], axis=0)
    return out
```

```python
from contextlib import ExitStack
import numpy as np
import concourse.bass as bass
import concourse.mybir as mybir
from concourse.bass_utils import run_bass_kernel_spmd

F32 = mybir.dt.float32
BF16 = mybir.dt.bfloat16
I32 = mybir.dt.int32
AF = mybir.ActivationFunctionType
ALU = mybir.AluOpType

S_LEN = 4096
D = 1024
NB = 32
NT = 8
EPS = 1e-6
QW = 1024
SEM_ROLL = 1000
PE_FILL = 1


class Res:
    __slots__ = ("name", "last_write", "readers")

    def __init__(self, name=""):
        self.name = name
        self.last_write = None
        self.readers = []


class Op:
    __slots__ = ("eng", "fn", "deps", "is_dma", "ndma", "signal", "sem", "val", "idx", "barrier")

    def __init__(self, eng, fn, ndma):
        self.eng = eng
        self.fn = fn
        self.deps = []
        self.is_dma = ndma > 0
        self.ndma = ndma
        self.signal = False
        self.sem = None
        self.val = 0
        self.idx = 0
        self.barrier = False


class Sched:
    def __init__(self, nc, n_dma_sems=8):
        self.nc = nc
        self.engs = {"pe": nc.tensor, "act": nc.scalar, "dve": nc.vector, "pool": nc.gpsimd, "sp": nc.sync}
        self.ops = []
        self.n_dma_sems = n_dma_sems

    def op(self, eng, fn, reads=(), writes=(), dma=0):
        o = Op(eng, fn, dma)
        o.idx = len(self.ops)
        deps = {}
        for r in reads:
            if r.last_write is not None:
                deps[r.last_write.idx] = r.last_write
        for w in writes:
            if w.last_write is not None:
                deps[w.last_write.idx] = w.last_write
            for rd in w.readers:
                deps[rd.idx] = rd
        for r in reads:
            r.readers.append(o)
        for w in writes:
            w.last_write = o
            w.readers = []
        o.deps = list(deps.values())
        self.ops.append(o)
        return o

    def barrier(self):
        for e in self.engs:
            o = Op(e, None, 0)
            o.idx = len(self.ops)
            o.barrier = True
            self.ops.append(o)

    def emit(self):
        nc = self.nc
        last_on = {}
        for o in self.ops:
            if o.barrier:
                for e, lo in last_on.items():
                    lo.signal = True
                continue
            best = {}
            dmas = []
            for d in o.deps:
                if d.is_dma:
                    dmas.append(d)
                else:
                    if d.eng == "pe" and o.eng == "pe":
                        continue
                    b = best.get(d.eng)
                    if b is None or d.idx > b.idx:
                        best[d.eng] = d
            o.deps = list(best.values()) + dmas
            for d in o.deps:
                d.signal = True
            if not o.is_dma:
                last_on[o.eng] = o
        tl = {}
        all_tl = []
        dma_pool = {}
        waited = {}
        nsem = [0]

        def new_sem(name):
            nsem[0] += 1
            return nc.alloc_semaphore(name=f"{name}_{nsem[0]}")

        def do_waits(engine, wd, waits):
            mw = {}
            for s, v in waits:
                k = id(s)
                if wd.get(k, 0) >= v:
                    continue
                if k not in mw or mw[k][1] < v:
                    mw[k] = (s, v)
            for k, (s, v) in mw.items():
                engine.wait_ge(s, v)
                wd[k] = v

        for o in self.ops:
            engine = self.engs[o.eng]
            wd = waited.setdefault(o.eng, {})
            if o.barrier:
                waits = [(t[0], t[1]) for t in all_tl if t[1] > 0]
                for pool in dma_pool.values():
                    for ent in pool[0]:
                        if ent is not None and ent[1] > 0:
                            waits.append((ent[0], ent[1]))
                    for ent in pool[2]:
                        waits.append((ent[0], ent[1]))
                do_waits(engine, wd, waits)
                continue
            waits = [(d.sem, d.val) for d in o.deps]
            if o.is_dma:
                pool = dma_pool.get(o.eng)
                if pool is None:
                    pool = [[None] * self.n_dma_sems, 0, []]
                    dma_pool[o.eng] = pool
                slot = pool[1] % self.n_dma_sems
                pool[1] += 1
                ent = pool[0][slot]
                if ent is not None and ent[1] > 0:
                    waits.append((ent[0], ent[1]))
                if ent is None or ent[1] + 16 * o.ndma >= SEM_ROLL:
                    if ent is not None:
                        pool[2].append(ent)
                    ent = [new_sem(f"dma_{o.eng}_{slot}"), 0]
                    pool[0][slot] = ent
                o.sem = ent[0]
                ent[1] += 16 * o.ndma
                o.val = ent[1]
                do_waits(engine, wd, waits)
                insts = o.fn()
                if not isinstance(insts, (list, tuple)):
                    insts = [insts]
                assert len(insts) == o.ndma, (len(insts), o.ndma)
                for ins in insts:
                    ins.then_inc(o.sem, 16)
            else:
                do_waits(engine, wd, waits)
                ins = o.fn()
                if isinstance(ins, (list, tuple)):
                    ins = ins[-1]
                if o.signal:
                    t = tl.get(o.eng)
                    if t is None or t[1] >= SEM_ROLL:
                        t = [new_sem(f"tl_{o.eng}"), 0]
                        tl[o.eng] = t
                        all_tl.append(t)
                    t[1] += 1
                    o.sem = t[0]
                    o.val = t[1]
                    ins.then_inc(o.sem, 1)
        for e, engine in self.engs.items():
            wd = waited.setdefault(e, {})
            waits = [(t[0], t[1]) for t in all_tl if t[1] > 0]
            for pool in dma_pool.values():
                for ent in pool[0]:
                    if ent is not None and ent[1] > 0:
                        waits.append((ent[0], ent[1]))
                for ent in pool[2]:
                    waits.append((ent[0], ent[1]))
            do_waits(engine, wd, waits)


def build_nc(layers=(0, 1), debug=False):
    nc = bass.Bass("TRN2", target_bir_lowering=False)
    S = Sched(nc)

    def din(name, shape, dt=F32):
        return nc.dram_tensor(name, list(shape), dt, kind="ExternalInput").ap()

    def dscr(name, shape, dt=BF16):
        kind = "ExternalOutput" if debug else "Internal"
        return nc.dram_tensor(name, list(shape), dt, kind=kind).ap()

    x_in = din("x", [S_LEN, D])
    pos_in = din("pos", [128, NB], I32)
    consts_in = din("consts", [128, 8 * 128])
    g0_pre = din("l0_pre_g", [128, 8])
    g0_post = din("l0_post_g", [D])
    w0_in = din("l0_w_in", [D, 3232])
    g0_qa = din("l0_q_a_g", [128, 3])
    w0_qb = din("l0_w_q_b", [384, 768])
    g0_kva = din("l0_kv_a_g", [128, 2])
    w0_kvb = din("l0_w_kv_b", [256, 1024])
    w0_out = din("l0_w_out", [D, D])
    g1_pre = din("l1_pre_g", [128, 8])
    g1_post = din("l1_post_g", [D])
    w1_in = din("l1_w_in", [D, 4112])
    b1_f = din("l1_b_f", [16])
    w1_out = din("l1_w_out", [D, D])
    y_out = nc.dram_tensor("y", [S_LEN, D], F32, kind="ExternalOutput").ap()

    X1 = dscr("X1", [S_LEN, D], F32)
    SQK = dscr("SQK", [8, 128, S_LEN])
    SG = dscr("SG", [8, 128, S_LEN])
    SV = dscr("SV", [S_LEN, 512])
    MQ = dscr("MQ", [8, 96, S_LEN])
    MK = dscr("MK", [8, 64, S_LEN])
    MKR = dscr("MKR", [32, S_LEN])
    MV = dscr("MV", [S_LEN, 512])
    OGT = dscr("OGT", [8, 128, S_LEN])
    FQ = dscr("FQ", [16, 65, S_LEN])
    FK = dscr("FK", [16, 64, S_LEN])
    FG = dscr("FG", [8, 128, S_LEN])
    FV = dscr("FV", [S_LEN, 1024])

    es = ExitStack()

    def sb(name, shape, dt, stack=None):
        return (stack or es).enter_context(nc.sbuf_tensor(name, list(shape), dt))

    def ps(name, shape, dt, stack):
        return stack.enter_context(nc.psum_tensor(name, list(shape), dt))

    cst_f = sb("cst_f", [128, 1024], F32)
    cst_b = sb("cst_b", [128, 1024], BF16)
    ident_b = cst_b[:, 0:128]
    tri_b = cst_b[:, 128:256]
    onesn_b = cst_b[:, 256:384]
    mle_b = cst_b[:, 384:512]
    mlt_b = cst_b[:, 512:640]
    ident_f = cst_f[:, 0:128]
    triC_f = cst_f[:, 640:768]
    ones_f = cst_f[:, 768:896]
    wout_b = sb("wout_b", [128, 2, 8, D], BF16)
    gpost = sb("gpost", [128, 2, D], F32)
    negc = sb("negc", [128, NB, 16], F32)
    mhalf = sb("mhalf", [128, 1], F32)
    R_cst = Res("cst")
    R_wout = Res("wout")
    R_gpost = Res("gpost")
    R_negc = Res("negc")
    R_mhalf = Res("mhalf")

    def dma(eng, out, in_, reads=(), writes=()):
        e = nc.sync if eng == "sp" else nc.scalar
        return S.op(eng, lambda: e.dma_start(out=out, in_=in_), reads=reads, writes=writes, dma=1)

    def dmas(eng, pairs, reads=(), writes=()):
        e = nc.sync if eng == "sp" else nc.scalar
        return S.op(eng, lambda: [e.dma_start(out=o, in_=i) for o, i in pairs], reads=reads, writes=writes,
                    dma=len(pairs))

    def mms(specs, reads, writes):
        def f():
            ins = None
            for sp4 in specs:
                (o, l, r, st, sp_) = sp4[:5]
                if len(sp4) > 5:
                    ins = nc.tensor.matmul(o, lhsT=l, rhs=r, start=st, stop=sp_, skip_group_check=True)
                else:
                    ins = nc.tensor.matmul(o, lhsT=l, rhs=r, start=st, stop=sp_)
            return ins
        return S.op("pe", f, reads=reads, writes=writes)

    def transposes(specs, reads, writes):
        def f():
            ins = None
            for (o, i, idn) in specs:
                ins = nc.tensor.transpose(o, i, idn)
            return ins
        return S.op("pe", f, reads=reads, writes=writes)

    def V(eng, method, kw, reads, writes):
        engine = S.engs[eng]
        return S.op(eng, lambda: getattr(engine, method)(**kw), reads=reads, writes=writes)

    def SQ(junk, R_jk, jkc, cols, kw, reads, writes):
        j = jkc[0] % 2
        jkc[0] += 1
        if cols == ":":
            o = junk[:, j, :]
        else:
            a, b_ = cols.split(":")
            o = junk[:, j, int(a):int(b_)]
        kw = dict(kw)
        kw["out"] = o
        return V("act", "activation", kw, reads, list(writes) + [R_jk[j]])

    dma("sp", cst_f[:, :], consts_in[:, :], writes=[R_cst])
    V("dve", "tensor_copy", dict(out=cst_b[:, :], in_=cst_f[:, :]), [R_cst], [R_cst])
    V("pool", "memset", dict(ap=mhalf[:, :], constant=-0.5), [], [R_mhalf])
    dmas("sp", [(gpost[:, 0, :], g0_post.partition_broadcast(128)),
                (gpost[:, 1, :], g1_post.partition_broadcast(128))], writes=[R_gpost])

    WCH = 2056

    def make_wstage(stack, tag):
        return (sb(f"wstg_{tag}", [128, 2, WCH], F32, stack), [Res(), Res()])

    def load_weight(dst, src, kc, ncols, g_src, stack, tag, wst):
        CH = WCH
        stg, R_stg = wst
        gt = None
        R_g = Res()
        if g_src is not None:
            gt = sb(f"wg_{tag}", [128, kc], F32, stack)
            dma("sp", gt[:, :], g_src[:, :], writes=[R_g])
        i = 0
        for c in range(kc):
            for c0 in range(0, ncols, CH):
                w = min(CH, ncols - c0)
                b = i % 2
                dma("sp", stg[:, b, 0:w], src[c * 128:(c + 1) * 128, c0:c0 + w], writes=[R_stg[b]])
                if gt is not None:
                    V("dve", "tensor_scalar", dict(
                        out=dst[:, c, c0:c0 + w], in0=stg[:, b, 0:w], scalar1=gt[:, c:c + 1], scalar2=None,
                        op0=ALU.mult), [R_stg[b], R_g], [])
                else:
                    V("dve", "tensor_copy", dict(
                        out=dst[:, c, c0:c0 + w], in_=stg[:, b, 0:w]), [R_stg[b]], [])
                i += 1

    with ExitStack() as st0:
        wst0 = make_wstage(st0, "p")
        load_weight(wout_b[:, 0], w0_out, 8, D, None, st0, "wo0", wst0)
        load_weight(wout_b[:, 1], w1_out, 8, D, None, st0, "wo1", wst0)
        S.barrier()

    def rstd_ops(ss, rstd, n, R_ss, R_rstd):
        V("dve", "tensor_scalar", dict(out=ss, in0=ss, scalar1=1.0 / n, scalar2=EPS,
                                                   op0=ALU.mult, op1=ALU.add), [R_ss], [R_ss])
        V("pool", "tensor_tensor", dict(out=rstd, in0=ss, in1=mhalf[:, :], op=ALU.pow), [R_ss, R_mhalf], [R_rstd])

    def phase_a0(xsrc):
        with ExitStack() as st:
            wbig = sb("a0_wbig", [128, 8, 3232], BF16, st)
            wqb = sb("a0_wqb", [128, 3, 768], BF16, st)
            wkvb = sb("a0_wkvb", [128, 2, 1024], BF16, st)
            wst = make_wstage(st, "a0")
            load_weight(wbig, w0_in, 8, 3232, g0_pre, st, "win0", wst)
            load_weight(wqb, w0_qb, 3, 768, g0_qa, st, "wqb", wst)
            load_weight(wkvb, w0_kvb, 2, 1024, g0_kva, st, "wkvb", wst)
            posi = sb("a0_posi", [128, NB], I32, st)
            posf = sb("a0_posf", [128, NB], F32, st)
            ang = sb("a0_ang", [128, 2, NB, 16], F32, st)
            tmpa = sb("a0_tmpa", [128, 2, NB, 16], F32, st)
            tmpi = sb("a0_tmpi", [128, 2, NB, 16], I32, st)
            trig = sb("a0_trig", [128, 2, NB, 16], F32, st)
            R_pos, R_ang, R_tmp, R_trig = Res(), Res(), Res(), Res()
            dma("sp", posi[:, :], pos_in[:, :], writes=[R_pos])
            V("dve", "tensor_copy", dict(out=posf[:, :], in_=posi[:, :]), [R_pos], [R_pos])
            TWO_PI = 2.0 * np.pi
            for j in range(16):
                invf = float(np.float32(10000.0) ** np.float32(-(2.0 * j) / 32.0))
                V("dve", "tensor_scalar", dict(
                    out=ang[:, 0, :, j], in0=posf[:, :], scalar1=invf, scalar2=None, op0=ALU.mult), [R_pos], [R_ang])
            V("dve", "tensor_scalar", dict(out=ang[:, 1], in0=ang[:, 0], scalar1=float(np.pi / 2),
                                                       scalar2=None, op0=ALU.add), [R_ang], [R_ang])
            V("dve", "tensor_scalar", dict(out=tmpa[:], in0=ang[:], scalar1=float(1.0 / TWO_PI),
                                                       scalar2=None, op0=ALU.mult), [R_ang], [R_tmp])
            V("dve", "tensor_copy", dict(out=tmpi[:], in_=tmpa[:]), [R_tmp], [R_tmp])
            V("dve", "tensor_copy", dict(out=tmpa[:], in_=tmpi[:]), [R_tmp], [R_tmp])
            V("dve", "scalar_tensor_tensor", dict(out=ang[:], in0=tmpa[:], scalar=float(-TWO_PI),
                                                              in1=ang[:], op0=ALU.mult, op1=ALU.add), [R_tmp, R_ang], [R_ang])
            V("dve", "tensor_scalar", dict(out=tmpa[:], in0=ang[:], scalar1=float(np.pi),
                                                       scalar2=float(-TWO_PI), op0=ALU.is_gt, op1=ALU.mult), [R_ang], [R_tmp])
            V("dve", "tensor_tensor", dict(out=ang[:], in0=ang[:], in1=tmpa[:], op=ALU.add), [R_ang, R_tmp], [R_ang])
            V("dve", "tensor_scalar", dict(out=tmpa[:], in0=ang[:], scalar1=float(-np.pi),
                                                       scalar2=float(TWO_PI), op0=ALU.is_lt, op1=ALU.mult), [R_ang], [R_tmp])
            V("dve", "tensor_tensor", dict(out=ang[:], in0=ang[:], in1=tmpa[:], op=ALU.add), [R_ang, R_tmp], [R_ang])
            V("act", "activation", dict(out=trig[:], in_=ang[:], func=AF.Sin), [R_ang], [R_trig])
            sin_t = trig[:, 0]
            cos_t = trig[:, 1]

            xt = sb("a0_xt", [128, 2, D], F32, st)
            junk = sb("a0_junk", [128, 2, D], BF16, st)
            R_jk = [Res(), Res()]
            jkc = [0]
            hb = sb("a0_hb", [128, 2, D], BF16, st)
            hT = sb("a0_hT", [128, 2, 8, 512], BF16, st)
            ssx = sb("a0_ssx", [128, 2, 4], F32, st)
            stg = sb("a0_stg", [128, 4, 512], BF16, st)
            qan = sb("a0_qan", [128, 2, 384], BF16, st)
            ckvn = sb("a0_ckvn", [128, 2, 256], BF16, st)
            krf = sb("a0_krf", [128, 2, 4, 16], F32, st)
            krb = sb("a0_krb", [128, 2, 32], BF16, st)
            qanT = sb("a0_qanT", [128, 2, 3, 512], BF16, st)
            ckvT = sb("a0_ckvT", [128, 2, 2, 512], BF16, st)
            krT = sb("a0_krT", [32, 2, 512], BF16, st)
            qrf = sb("a0_qrf", [128, 2, 4, 8, 16], F32, st)
            qrb = sb("a0_qrb", [128, 2, 256], BF16, st)
            qrT = sb("a0_qrT", [128, 2, 2, 512], BF16, st)
            pT = [ps(f"a0_pT{i}", [128, 1024], BF16, st) for i in range(2)]
            pF = [ps(f"a0_pF{i}", [128, 512], F32, st) for i in range(3)]
            pK = [ps(f"a0_pK{i}", [128, 512], F32, st) for i in range(3)]
            R_xt = [Res(), Res()]
            R_hb = [Res(), Res()]
            R_hT = [Res(), Res()]
            R_ss = [[Res() for _ in range(4)] for _ in range(2)]
            R_rs = [[Res() for _ in range(4)] for _ in range(2)]
            R_junk = Res()
            R_stg = [Res() for _ in range(4)]
            R_sil = [Res(), Res()]
            R_qan, R_ckvn, R_krf, R_krb = [Res(), Res()], [Res(), Res()], [Res(), Res()], [Res(), Res()]
            R_qanT, R_ckvT, R_krT = [Res(), Res()], [Res(), Res()], [Res(), Res()]
            R_qrf, R_qrb, R_qrT = [Res(), Res()], [Res(), Res()], [Res(), Res()]
            R_pT = [Res(), Res()]
            R_pF = [Res() for _ in range(3)]
            R_pK = [Res() for _ in range(3)]
            cnt = {"pT": 0, "pF": 0, "pK": 0, "stg": 0, "sil": 0, "sm": 0}

            def nxt(k, n):
                v = cnt[k] % n
                cnt[k] += 1
                return v

            def store_bf(src_ps, rows, ncols, dsts, R_src, eng="dve"):
                si = nxt("stg", 4)
                if eng == "dve":
                    V("dve", "tensor_copy", dict(out=stg[0:rows, si, 0:ncols], in_=src_ps), [R_src], [R_stg[si]])
                else:
                    V("act", "copy", dict(out=stg[0:rows, si, 0:ncols], in_=src_ps), [R_src], [R_stg[si]])
                dmas("act", [(d, stg[r0:r1, si, 0:ncols]) for (d, r0, r1) in dsts], reads=[R_stg[si]])

            def norm_block(T, sbk):
                hs = T % 2
                tb = T * 4 + sbk
                xi = tb % 2
                dma("sp", xt[:, xi, :], xsrc[tb * 128:(tb + 1) * 128, :], writes=[R_xt[xi]])
                ss = ssx[:, xi, 0:1]
                rs = ssx[:, xi, 1:2]
                SQ(junk, R_jk, jkc, ":", dict(in_=xt[:, xi, :],
                                                                     func=AF.Square, accum_out=ss), [R_xt[xi]], [R_ss[xi][0]])
                rstd_ops(ss, rs, D, R_ss[xi][0], R_rs[xi][0])
                V("dve", "tensor_scalar", dict(
                    out=hb[:, xi, :], in0=xt[:, xi, :], scalar1=rs, scalar2=None, op0=ALU.mult), [R_xt[xi], R_rs[xi][0]], [R_hb[xi]])
                pi = nxt("pT", 2)
                transposes([(pT[pi][:, fc * 128:(fc + 1) * 128], hb[:, xi, fc * 128:(fc + 1) * 128], ident_b)
                            for fc in range(8)], [R_hb[xi], R_cst], [R_pT[pi]])
                V("dve", "tensor_copy", dict(
                    out=hT[:, hs, :, sbk * 128:(sbk + 1) * 128],
                    in_=pT[pi][:, :].rearrange("p (c t) -> p c t", c=8)), [R_pT[pi]], [R_hT[hs]])
            for sbk in range(4):
                norm_block(0, sbk)
            for T in range(NT):
                hs = T % 2
                for sbk in range(4):
                    tb = T * 4 + sbk
                    tsl = slice(sbk * 128, (sbk + 1) * 128)
                    k = nxt("pK", 3)
                    mms([(pK[k][:, 0:512], hT[:, hs, fc, tsl], wbig[:, fc, 2048:2560], fc == 0, fc == 7)
                         for fc in range(8)], [R_hT[hs]], [R_pK[k]])
                    store_bf(pK[k][:, 0:512], 128, 512, [(SV[tb * 128:(tb + 1) * 128, :], 0, 128)], R_pK[k], "act")
                    k = nxt("pK", 3)
                    mms([(pK[k][:, 0:384], hT[:, hs, fc, tsl], wbig[:, fc, 2560:2944], fc == 0, fc == 7)
                         for fc in range(8)], [R_hT[hs]], [R_pK[k]])
                    sm = nxt("sm", 2)
                    ss = ssx[:, sm, 2:3]
                    rs = ssx[:, sm, 3:4]
                    SQ(junk, R_jk, jkc, "0:384", dict(in_=pK[k][:, 0:384],
                                                                       func=AF.Square, accum_out=ss), [R_pK[k]], [R_ss[sm][1]])
                    rstd_ops(ss, rs, 384, R_ss[sm][1], R_rs[sm][1])
                    V("dve", "tensor_scalar", dict(
                        out=qan[:, sm, :], in0=pK[k][:, 0:384], scalar1=rs, scalar2=None, op0=ALU.mult), [R_pK[k], R_rs[sm][1]], [R_qan[sm]])
                    pi = nxt("pT", 2)
                    transposes([(pT[pi][:, j * 128:(j + 1) * 128], qan[:, sm, j * 128:(j + 1) * 128], ident_b)
                                for j in range(3)], [R_qan[sm], R_cst], [R_pT[pi]])
                    V("dve", "tensor_copy", dict(
                        out=qanT[:, hs, :, tsl], in_=pT[pi][:, 0:384].rearrange("p (c t) -> p c t", c=3)), [R_pT[pi]], [R_qanT[hs]])
                    k = nxt("pK", 3)
                    mms([(pK[k][:, 0:288], hT[:, hs, fc, tsl], wbig[:, fc, 2944:3232], fc == 0, fc == 7)
                         for fc in range(8)], [R_hT[hs]], [R_pK[k]])
                    ssc = kv_stats[:, sm, 0:1]
                    rsc = kv_stats[:, sm, 1:2]
                    SQ(junk, R_jk, jkc, "0:256", dict(in_=pK[k][:, 0:256],
                                                                         func=AF.Square, accum_out=ssc), [R_pK[k]], [R_kvs[sm][0]])
                    rstd_ops(ssc, rsc, 256, R_kvs[sm][0], R_kvs[sm][1])
                    V("dve", "tensor_scalar", dict(
                        out=ckvn[:, sm, :], in0=pK[k][:, 0:256], scalar1=rsc, scalar2=None, op0=ALU.mult), [R_pK[k], R_kvs[sm][1]], [R_ckvn[sm]])
                    cs = cos_t[:, tb, :]
                    sn = sin_t[:, tb, :]
                    x1 = pK[k][:, 256:272]
                    x2 = pK[k][:, 272:288]
                    V("dve", "tensor_tensor", dict(
                        out=krf[:, sm, 0, :], in0=x1, in1=cs, op=ALU.mult), [R_pK[k], R_trig], [R_krf[sm]])
                    V("dve", "tensor_tensor", dict(
                        out=krf[:, sm, 1, :], in0=x2, in1=sn, op=ALU.mult), [R_pK[k], R_trig], [R_krf[sm]])
                    V("dve", "tensor_tensor", dict(
                        out=krf[:, sm, 2, :], in0=x2, in1=cs, op=ALU.mult), [R_pK[k], R_trig], [R_krf[sm]])
                    V("dve", "tensor_tensor", dict(
                        out=krf[:, sm, 3, :], in0=x1, in1=sn, op=ALU.mult), [R_pK[k], R_trig], [R_krf[sm]])
                    V("dve", "tensor_tensor", dict(
                        out=krb[:, sm, 0:16], in0=krf[:, sm, 0, :], in1=krf[:, sm, 1, :], op=ALU.subtract), [R_krf[sm]], [R_krb[sm]])
                    V("dve", "tensor_tensor", dict(
                        out=krb[:, sm, 16:32], in0=krf[:, sm, 2, :], in1=krf[:, sm, 3, :], op=ALU.add), [R_krf[sm]], [R_krb[sm]])
                    pi = nxt("pT", 2)
                    transposes([(pT[pi][:, j * 128:(j + 1) * 128], ckvn[:, sm, j * 128:(j + 1) * 128], ident_b)
                                for j in range(2)] +
                               [(pT[pi][0:32, 256:384], krb[:, sm, :], ident_b)],
                               [R_ckvn[sm], R_krb[sm], R_cst], [R_pT[pi]])
                    V("dve", "tensor_copy", dict(
                        out=ckvT[:, hs, :, tsl], in_=pT[pi][:, 0:256].rearrange("p (c t) -> p c t", c=2)), [R_pT[pi]], [R_ckvT[hs]])
                    V("dve", "tensor_copy", dict(
                        out=krT[:, hs, tsl], in_=pT[pi][0:32, 256:384]), [R_pT[pi]], [R_krT[hs]])
                for sbk in range(4):
                    tb = T * 4 + sbk
                    tsl = slice(sbk * 128, (sbk + 1) * 128)
                    k = nxt("pK", 3)
                    mms([(pK[k][:, 0:256], qanT[:, hs, j, tsl], wqb[:, j, 512:768], j == 0, j == 2)
                         for j in range(3)], [R_qanT[hs]], [R_pK[k]])
                    sm = nxt("sm", 2)
                    pv = pK[k][:, 0:256].rearrange("p (h r) -> p h r", h=8)
                    x1 = pv[:, :, 0:16]
                    x2 = pv[:, :, 16:32]
                    cs = cos_t[:, tb, :].unsqueeze(1).broadcast_to([128, 8, 16])
                    sn = sin_t[:, tb, :].unsqueeze(1).broadcast_to([128, 8, 16])
                    for idx, (a, b_) in enumerate([(x1, cs), (x2, sn), (x2, cs), (x1, sn)]):
                        V("dve", "tensor_tensor", dict(
                            out=qrf[:, sm, idx], in0=a, in1=b_, op=ALU.mult), [R_pK[k], R_trig], [R_qrf[sm]])
                    qv = qrb[:, sm, :].rearrange("p (h r) -> p h r", h=8)
                    V("dve", "tensor_tensor", dict(
                        out=qv[:, :, 0:16], in0=qrf[:, sm, 0], in1=qrf[:, sm, 1], op=ALU.subtract), [R_qrf[sm]], [R_qrb[sm]])
                    V("dve", "tensor_tensor", dict(
                        out=qv[:, :, 16:32], in0=qrf[:, sm, 2], in1=qrf[:, sm, 3], op=ALU.add), [R_qrf[sm]], [R_qrb[sm]])
                    pi = nxt("pT", 2)
                    transposes([(pT[pi][:, j * 128:(j + 1) * 128], qrb[:, sm, j * 128:(j + 1) * 128], ident_b)
                                for j in range(2)], [R_qrb[sm], R_cst], [R_pT[pi]])
                    V("dve", "tensor_copy", dict(
                        out=qrT[:, hs, :, tsl], in_=pT[pi][:, 0:256].rearrange("p (c t) -> p c t", c=2)), [R_pT[pi]], [R_qrT[hs]])
                    k = nxt("pK", 3)
                    mms([(pK[k][:, 0:512], ckvT[:, hs, j, tsl], wkvb[:, j, 512:1024], j == 0, j == 1)
                         for j in range(2)], [R_ckvT[hs]], [R_pK[k]])
                    store_bf(pK[k][:, 0:512], 128, 512, [(MV[tb * 128:(tb + 1) * 128, :], 0, 128)], R_pK[k], "act")
                tcs = slice(T * 512, (T + 1) * 512)
                for c in range(16):
                    f = nxt("pF", 3)
                    mms([(pF[f][:, :], wbig[:, fc, c * 128:(c + 1) * 128], hT[:, hs, fc, :], fc == 0, fc == 7)
                         for fc in range(8)], [R_hT[hs]], [R_pF[f]])
                    if c < 8:
                        store_bf(pF[f][:, :], 128, 512, [(SQK[c][:, tcs], 0, 128)], R_pF[f])
                    else:
                        si = nxt("stg", 4)
                        V("act", "activation", dict(out=stg[:, si, :], in_=pF[f][:, :],
                                                                           func=AF.Silu), [R_pF[f]], [R_stg[si]])
                        dma("act", SG[c - 8][:, tcs], stg[:, si, :], reads=[R_stg[si]])
                    if c % 4 == 3 and T + 1 < NT:
                        norm_block(T + 1, c // 4)
                for c in range(4):
                    f = nxt("pF", 3)
                    mms([(pF[f][:, :], wqb[:, j, c * 128:(c + 1) * 128], qanT[:, hs, j, :], j == 0, j == 2)
                         for j in range(3)], [R_qanT[hs]], [R_pF[f]])
                    store_bf(pF[f][:, :], 128, 512, [(MQ[2 * c][0:64, tcs], 0, 64), (MQ[2 * c + 1][0:64, tcs], 64, 128)],
                             R_pF[f])
                    f = nxt("pF", 3)
                    mms([(pF[f][:, :], wkvb[:, j, c * 128:(c + 1) * 128], ckvT[:, hs, j, :], j == 0, j == 1)
                         for j in range(2)], [R_ckvT[hs]], [R_pF[f]])
                    store_bf(pF[f][:, :], 128, 512, [(MK[2 * c][:, tcs], 0, 64), (MK[2 * c + 1][:, tcs], 64, 128)],
                             R_pF[f])
                dmas("act", [(MQ[4 * j + i][64:96, tcs], qrT[i * 32:(i + 1) * 32, hs, j, :])
                              for j in range(2) for i in range(4)], reads=[R_qrT[hs]])
                dma("act", MKR[:, tcs], krT[:, hs, :], reads=[R_krT[hs]])
            S.barrier()

    kv_stats = sb("kv_stats", [128, 2, 2], F32)
    R_kvs = [[Res(), Res()], [Res(), Res()]]

    def phase_attn(heads, kind_tag):
        with ExitStack() as st:
            QT = sb(f"{kind_tag}_QT", [128, 2, S_LEN], BF16, st)
            KT = sb(f"{kind_tag}_KT", [128, 2, S_LEN], BF16, st)
            VT = sb(f"{kind_tag}_VT", [128, 2, NB, 128], BF16, st)
            SGt = sb(f"{kind_tag}_SG", [128, 2, QW], BF16, st)
            PT = sb(f"{kind_tag}_PT", [128, 3, QW], BF16, st)
            Ef = sb(f"{kind_tag}_Ef", [128, 2, QW], F32, st)
            SPt = sb(f"{kind_tag}_SP", [128, 2, QW], BF16, st)
            Ssum = sb(f"{kind_tag}_Ss", [128, 2, QW], BF16, st)
            rl = sb(f"{kind_tag}_rl", [128, 2, QW], F32, st)
            t1 = sb(f"{kind_tag}_t1", [128, 2, QW], F32, st)
            ogt = sb(f"{kind_tag}_og", [128, 2, QW], BF16, st)
            any_sb = any(h["kind"] == "sb" for h in heads)
            nZ = 3
            nO = 1
            Z = [ps(f"{kind_tag}_Z{i}", [128, QW], F32, st) for i in range(nZ)]
            O = [ps(f"{kind_tag}_O{i}", [128, QW], F32, st) for i in range(nO)]
            R_QT, R_KT, R_VT = [Res(), Res()], [Res(), Res()], [Res(), Res()]
            R_SG = [Res(), Res()]
            R_PT = [Res() for _ in range(3)]
            R_Ef, R_SP, R_Ss = [Res(), Res()], [Res(), Res()], [Res(), Res()]
            R_rl, R_t1, R_og = [Res(), Res()], [Res(), Res()], [Res(), Res()]
            R_Z = [Res() for _ in range(nZ)]
            R_O = [Res() for _ in range(nO)]
            V("pool", "memset", dict(ap=VT[:, 0, :, 64:128], constant=1.0), [], [R_VT[0]])
            V("pool", "memset", dict(ap=VT[:, 1, :, 0:64], constant=1.0), [], [R_VT[1]])
            if any(h["kind"] == "fox" for h in heads):
                V("pool", "memset", dict(ap=KT[64:65, 0, :], constant=1.0), [], [R_KT[0]])
                V("pool", "memset", dict(ap=KT[64:65, 1, :], constant=1.0), [], [R_KT[1]])

            def load_head(hi):
                h = heads[hi]
                b = hi % 2
                par = h["par"]
                dmas("sp", [(QT[r0:r1, b, :], src) for (src, r0, r1) in h["q"]], writes=[R_QT[b]])
                dmas("sp", [(KT[r0:r1, b, :], src) for (src, r0, r1) in h["k"]], writes=[R_KT[b]])
                vsrc = h["v"]
                vc = slice(0, 64) if par == 0 else slice(64, 128)
                vv = vsrc.rearrange("(blk p) c -> p blk c", p=128)
                dmas("sp", [(VT[:, b, q4 * 8:(q4 + 1) * 8, vc], vv[:, q4 * 8:(q4 + 1) * 8, :]) for q4 in range(NB // 8)],
                     writes=[R_VT[b]])

            zc = [0]
            oc = [0]
            sgc = [0]
            ptc = [0]

            def compute_head(hi):
                h = heads[hi]
                b = hi % 2
                kind = h["kind"]
                par = h["par"]
                kd = h["kd"]
                scale = h["scale"]
                rowsO = slice(0, 64) if par == 0 else slice(64, 128)
                rowsL = slice(64, 128) if par == 0 else slice(0, 64)
                mask = mlt_b if kind == "sb" else mle_b
                for t in range(S_LEN // QW):
                    q0 = t * QW
                    nq = QW // 128
                    sgi = sgc[0] % 2
                    sgc[0] += 1
                    dma("sp", SGt[rowsO, sgi, :], h["g"][rowsO, q0:q0 + QW], writes=[R_SG[sgi]])
                    oi = oc[0] % nO
                    oc[0] += 1
                    pairs = []
                    for kb in range(nq * t + nq - 1, -1, -1):
                        c0 = max(0, kb - nq * t) * 128
                        pairs.append((kb, c0, kb >= nq * t))
                    first_bank = [True, True]
                    if kind == "sb":
                        V("pool", "memset", dict(ap=Ssum[:, 0, :], constant=0.0), [], [R_Ss[0]])
                        V("pool", "memset", dict(ap=Ssum[:, 1, :], constant=0.0), [], [R_Ss[1]])
                    zis = []

                    def banks(c0):
                        out = []
                        if c0 < 512:
                            out.append((c0, 512))
                        out.append((max(c0, 512), QW))
                        return out

                    def emit_qk(i):
                        kb, c0, diag = pairs[i]
                        zi = zc[0] % nZ
                        zc[0] += 1
                        zis.append(zi)
                        qspecs = [(Z[zi][:, lo:hi], KT[0:kd, b, kb * 128:(kb + 1) * 128], QT[0:kd, b, q0 + lo:q0 + hi],
                                   True, True) for (lo, hi) in banks(c0)]
                        if kind != "sb" and PE_FILL:
                            qspecs = qspecs + qspecs[-1:] * PE_FILL
                        mms(qspecs, [R_KT[b], R_QT[b]], [R_Z[zi]])

                    pts = {}

                    def emit_p(i):
                        kb, c0, diag = pairs[i]
                        zi = zis[i]
                        if kind != "sb":
                            pi = ptc[0] % 3
                            ptc[0] += 1
                            pts[i] = pi
                            if kind == "fox":
                                bias = negc[:, kb, h["hidx"]:h["hidx"] + 1]
                                V("act", "activation", dict(out=PT[:, pi, c0:QW], in_=Z[zi][:, c0:QW],
                                                                        func=AF.Exp, bias=bias, scale=scale), [R_Z[zi], R_negc], [R_PT[pi]])
                            else:
                                V("act", "activation", dict(out=PT[:, pi, c0:QW], in_=Z[zi][:, c0:QW],
                                                                        func=AF.Exp, scale=scale), [R_Z[zi]], [R_PT[pi]])
                            if diag:
                                V("dve", "tensor_tensor", dict(out=PT[:, pi, c0:c0 + 128],
                                                                           in0=PT[:, pi, c0:c0 + 128], in1=mask,
                                                                           op=ALU.mult), [R_PT[pi], R_cst], [R_PT[pi]])
                        else:
                            ei = i % 2
                            V("act", "activation", dict(out=Ef[:, ei, c0:QW], in_=Z[zi][:, c0:QW],
                                                                    func=AF.Exp, scale=scale), [R_Z[zi]], [R_Ef[ei]])
                            V("act", "activation", dict(out=SPt[:, ei, c0:QW], in_=Ef[:, ei, c0:QW],
                                                                    func=AF.Ln, bias=1.0, scale=1.0), [R_Ef[ei]], [R_SP[ei]])
                            if diag:
                                V("dve", "tensor_tensor", dict(out=SPt[:, ei, c0:c0 + 128],
                                                                           in0=SPt[:, ei, c0:c0 + 128], in1=mask,
                                                                           op=ALU.mult), [R_SP[ei], R_cst], [R_SP[ei]])

                    def emit_tri(i):
                        kb, c0, diag = pairs[i]
                        zi = zis[i]
                        ei = i % 2
                        si = i % 2
                        specs = []
                        for (lo, hi) in banks(c0):
                            specs.append((Z[zi][:, lo:hi], tri_b, SPt[:, ei, lo:hi], False, i == 0, "skip"))
                            if i > 0:
                                specs.append((Z[zi][:, lo:hi], onesn_b, Ssum[:, si, lo:hi], False, True, "skip"))
                        mms(specs, [R_SP[ei], R_Ss[si], R_cst], [R_Z[zi]])
                        if i + 1 < len(pairs):
                            V("dve", "tensor_tensor", dict(out=Ssum[:, 1 - si, c0:QW],
                                                                       in0=Ssum[:, si, c0:QW], in1=SPt[:, ei, c0:QW],
                                                                       op=ALU.add), [R_Ss[si], R_SP[ei]], [R_Ss[1 - si]])

                    def emit_w(i):
                        kb, c0, diag = pairs[i]
                        zi = zis[i]
                        pi = ptc[0] % 3
                        ptc[0] += 1
                        pts[i] = pi
                        V("act", "activation", dict(out=PT[:, pi, c0:QW], in_=Z[zi][:, c0:QW],
                                                                func=AF.Exp, scale=scale), [R_Z[zi]], [R_PT[pi]])
                        if diag:
                            V("dve", "tensor_tensor", dict(out=PT[:, pi, c0:c0 + 128],
                                                                       in0=PT[:, pi, c0:c0 + 128], in1=mask,
                                                                       op=ALU.mult), [R_PT[pi], R_cst], [R_PT[pi]])

                    def emit_pv(i):
                        kb, c0, diag = pairs[i]
                        pi = pts[i]
                        specs = []
                        last = (i == len(pairs) - 1)
                        for (lo, hi) in banks(c0):
                            bk = 0 if lo < 512 else 1
                            if kind == "sb":
                                lhsT = VT[:, b, kb, rowsO]
                                out = O[oi][rowsO, lo:hi]
                            else:
                                lhsT = VT[:, b, kb, :]
                                out = O[oi][:, lo:hi]
                            specs.append((out, lhsT, PT[:, pi, lo:hi], first_bank[bk], last, "skip"))
                            first_bank[bk] = False
                        mms(specs, [R_PT[pi], R_VT[b]], [R_O[oi]])

                    n = len(pairs)
                    for i in range(min(3, n)):
                        emit_qk(i)
                    if kind != "sb":
                        for i in range(n):
                            emit_p(i)
                            emit_pv(i)
                            if i + 3 < n:
                                emit_qk(i + 3)
                    else:
                        for i in range(min(2, n)):
                            emit_p(i)
                        emit_tri(0)
                        for i in range(n):
                            emit_w(i)
                            if i + 2 < n:
                                emit_p(i + 2)
                            if i + 1 < n:
                                emit_tri(i + 1)
                            emit_pv(i)
                            if i + 3 < n:
                                emit_qk(i + 3)
                    ri = t % 2
                    if kind == "sb":
                        V("dve", "tensor_tensor", dict(out=ogt[rowsO, ri, :], in0=O[oi][rowsO, :],
                                                                   in1=SGt[rowsO, sgi, :], op=ALU.mult), [R_O[oi], R_SG[sgi]], [R_og[ri]])
                    else:
                        V("dve", "reciprocal", dict(out=rl[rowsO, ri, :], in_=O[oi][rowsL, :]), [R_O[oi]], [R_rl[ri]])
                        V("dve", "tensor_tensor", dict(out=t1[rowsO, ri, :], in0=O[oi][rowsO, :],
                                                                   in1=rl[rowsO, ri, :], op=ALU.mult), [R_O[oi], R_rl[ri]], [R_t1[ri]])
                        V("pool", "tensor_tensor", dict(out=ogt[rowsO, ri, :], in0=t1[rowsO, ri, :],
                                                                    in1=SGt[rowsO, sgi, :], op=ALU.mult), [R_t1[ri], R_SG[sgi]], [R_og[ri]])
                    dma("sp", h["o"][rowsO, q0:q0 + QW], ogt[rowsO, ri, :], reads=[R_og[ri]])

            load_head(0)
            for hi in range(len(heads)):
                if hi + 1 < len(heads):
                    load_head(hi + 1)
                compute_head(hi)
            S.barrier()

    def phase_c(layer, xsrc, dst):
        with ExitStack() as st:
            og = sb(f"c{layer}_og", [128, 2, 8, 512], BF16, st)
            xt = sb(f"c{layer}_xt", [128, 2, D], F32, st)
            yt = sb(f"c{layer}_yt", [128, 2, D], F32, st)
            junk = sb(f"c{layer}_junk", [128, 2, D], BF16, st)
            R_jk = [Res(), Res()]
            jkc = [0]
            stt = sb(f"c{layer}_st", [128, 2, 2], F32, st)
            Y = [ps(f"c{layer}_Y{i}", [128, D], F32, st) for i in range(3)]
            R_ogc, R_xt, R_yt = [Res(), Res()], [Res(), Res()], [Res(), Res()]
            R_junk = Res()
            R_ss, R_rs = [Res(), Res()], [Res(), Res()]
            R_Y = [Res() for _ in range(3)]
            for T in range(NT):
                oi = T % 2
                dma("sp", og[:, oi], OGT[:, :, T * 512:(T + 1) * 512].rearrange("c p t -> p c t"),
                    writes=[R_ogc[oi]])
                for sbk in range(4):
                    tb = T * 4 + sbk
                    xi = tb % 2
                    yi = tb % 3
                    dma("sp", xt[:, xi, :], xsrc[tb * 128:(tb + 1) * 128, :], writes=[R_xt[xi]])
                    mms([(Y[yi][:, hf * 512:(hf + 1) * 512], og[:, oi, fc, sbk * 128:(sbk + 1) * 128],
                          wout_b[:, layer, fc, hf * 512:(hf + 1) * 512], fc == 0, fc == 7)
                         for hf in range(2) for fc in range(8)], [R_ogc[oi], R_wout], [R_Y[yi]])
                    ss = stt[:, xi, 0:1]
                    rs = stt[:, xi, 1:2]
                    SQ(junk, R_jk, jkc, ":", dict(in_=Y[yi][:, :],
                                                                         func=AF.Square, accum_out=ss), [R_Y[yi]], [R_ss[xi]])
                    rstd_ops(ss, rs, D, R_ss[xi], R_rs[xi])
                    V("dve", "scalar_tensor_tensor", dict(
                        out=yt[:, xi, :], in0=Y[yi][:, :], scalar=rs, in1=gpost[:, layer, :],
                        op0=ALU.mult, op1=ALU.mult), [R_Y[yi], R_rs[xi], R_gpost], [R_yt[xi]])
                    V("pool", "tensor_tensor", dict(out=yt[:, xi, :], in0=yt[:, xi, :],
                                                                       in1=xt[:, xi, :], op=ALU.add), [R_yt[xi], R_xt[xi]], [R_yt[xi]])
                    dma("act", dst[tb * 128:(tb + 1) * 128, :], yt[:, xi, :], reads=[R_yt[xi]])
            S.barrier()

    def phase_a1(xsrc):
        with ExitStack() as st:
            wbig = sb("a1_wbig", [128, 8, 4112], BF16, st)
            wst = make_wstage(st, "a1")
            load_weight(wbig, w1_in, 8, 4112, g1_pre, st, "win1", wst)
            xt = sb("a1_xt", [128, 2, D], F32, st)
            junk = sb("a1_junk", [128, 2, D], BF16, st)
            R_jk = [Res(), Res()]
            jkc = [0]
            hb = sb("a1_hb", [128, 2, D], BF16, st)
            hT = sb("a1_hT", [128, 2, 8, 512], BF16, st)
            ssx = sb("a1_ssx", [128, 2, 2], F32, st)
            stg = sb("a1_stg", [128, 4, 512], BF16, st)
            fl = sb("a1_fl", [128, NB, 16], F32, st)
            fl2 = sb("a1_fl2", [128, NB, 16], F32, st)
            bfb = sb("a1_bfb", [128, 16], F32, st)
            carry = sb("a1_carry", [128, NB, 16], F32, st)
            mrow = sb("a1_mrow", [16, S_LEN], BF16, st)
            pT = [ps(f"a1_pT{i}", [128, 1024], BF16, st) for i in range(2)]
            pF = [ps(f"a1_pF{i}", [128, 512], F32, st) for i in range(3)]
            pK = [ps(f"a1_pK{i}", [128, 512], F32, st) for i in range(3)]
            R_xt, R_hb, R_hT = [Res(), Res()], [Res(), Res()], [Res(), Res()]
            R_ss, R_rs = [Res(), Res()], [Res(), Res()]
            R_junk = Res()
            R_stg = [Res() for _ in range(4)]
            R_pT = [Res(), Res()]
            R_pF = [Res() for _ in range(3)]
            R_pK = [Res() for _ in range(3)]
            R_fl, R_fl2, R_bfb, R_carry, R_mrow = Res(), Res(), Res(), Res(), Res()
            cnt = {"pT": 0, "pF": 0, "pK": 0, "stg": 0}

            def nxt(k, n):
                v = cnt[k] % n
                cnt[k] += 1
                return v

            def store_bf(src_ps, rows, ncols, dsts, R_src, eng="dve"):
                si = nxt("stg", 4)
                if eng == "dve":
                    V("dve", "tensor_copy", dict(out=stg[0:rows, si, 0:ncols], in_=src_ps), [R_src], [R_stg[si]])
                else:
                    V("act", "copy", dict(out=stg[0:rows, si, 0:ncols], in_=src_ps), [R_src], [R_stg[si]])
                dmas("act", [(d, stg[r0:r1, si, 0:ncols]) for (d, r0, r1) in dsts], reads=[R_stg[si]])

            dma("sp", bfb[:, :], b1_f.partition_broadcast(128), writes=[R_bfb])
            def norm_block(T, sbk):
                hs = T % 2
                tb = T * 4 + sbk
                xi = tb % 2
                dma("sp", xt[:, xi, :], xsrc[tb * 128:(tb + 1) * 128, :], writes=[R_xt[xi]])
                ss = ssx[:, xi, 0:1]
                rs = ssx[:, xi, 1:2]
                SQ(junk, R_jk, jkc, ":", dict(in_=xt[:, xi, :],
                                                                     func=AF.Square, accum_out=ss), [R_xt[xi]], [R_ss[xi]])
                rstd_ops(ss, rs, D, R_ss[xi], R_rs[xi])
                V("dve", "tensor_scalar", dict(
                    out=hb[:, xi, :], in0=xt[:, xi, :], scalar1=rs, scalar2=None, op0=ALU.mult), [R_xt[xi], R_rs[xi]], [R_hb[xi]])
                pi = nxt("pT", 2)
                transposes([(pT[pi][:, fc * 128:(fc + 1) * 128], hb[:, xi, fc * 128:(fc + 1) * 128], ident_b)
                            for fc in range(8)], [R_hb[xi], R_cst], [R_pT[pi]])
                V("dve", "tensor_copy", dict(
                    out=hT[:, hs, :, sbk * 128:(sbk + 1) * 128],
                    in_=pT[pi][:, :].rearrange("p (c t) -> p c t", c=8)), [R_pT[pi]], [R_hT[hs]])
            for sbk in range(4):
                norm_block(0, sbk)
            for T in range(NT):
                hs = T % 2
                tcs = slice(T * 512, (T + 1) * 512)
                for sbk in range(4):
                    tb = T * 4 + sbk
                    tsl = slice(sbk * 128, (sbk + 1) * 128)
                    for vh in range(2):
                        k = nxt("pK", 3)
                        mms([(pK[k][:, 0:512], hT[:, hs, fc, tsl], wbig[:, fc, 3072 + vh * 512:3072 + (vh + 1) * 512],
                              fc == 0, fc == 7) for fc in range(8)], [R_hT[hs]], [R_pK[k]])
                        store_bf(pK[k][:, 0:512], 128, 512,
                                 [(FV[tb * 128:(tb + 1) * 128, vh * 512:(vh + 1) * 512], 0, 128)], R_pK[k],
                                 "act" if vh == 0 else "dve")
                    k = nxt("pK", 3)
                    mms([(pK[k][:, 0:16], hT[:, hs, fc, tsl], wbig[:, fc, 4096:4112], fc == 0, fc == 7)
                         for fc in range(8)], [R_hT[hs]], [R_pK[k]])
                    V("dve", "tensor_tensor", dict(out=fl[:, tb, :], in0=pK[k][:, 0:16],
                                                                           in1=bfb[:, :], op=ALU.add), [R_pK[k], R_bfb], [R_fl])
                for c in range(24):
                    f = nxt("pF", 3)
                    mms([(pF[f][:, :], wbig[:, fc, c * 128:(c + 1) * 128], hT[:, hs, fc, :], fc == 0, fc == 7)
                         for fc in range(8)], [R_hT[hs]], [R_pF[f]])
                    if c < 8:
                        store_bf(pF[f][:, :], 128, 512, [(FQ[2 * c][0:64, tcs], 0, 64), (FQ[2 * c + 1][0:64, tcs], 64, 128)],
                                 R_pF[f], "dve" if c % 2 else "act")
                    elif c < 16:
                        cc = c - 8
                        store_bf(pF[f][:, :], 128, 512, [(FK[2 * cc][:, tcs], 0, 64), (FK[2 * cc + 1][:, tcs], 64, 128)],
                                 R_pF[f], "dve" if c % 2 else "act")
                    else:
                        si = nxt("stg", 4)
                        V("act", "activation", dict(out=stg[:, si, :], in_=pF[f][:, :],
                                                                           func=AF.Silu), [R_pF[f]], [R_stg[si]])
                        dma("act", FG[c - 16][:, tcs], stg[:, si, :], reads=[R_stg[si]])
                    if c % 6 == 5 and T + 1 < NT:
                        norm_block(T + 1, c // 6)
            flv = fl[:, :, :].rearrange("p b h -> p (b h)")
            fl2v = fl2[:, :, :].rearrange("p b h -> p (b h)")
            V("act", "activation", dict(out=fl2v, in_=flv, func=AF.Exp, scale=-1.0), [R_fl], [R_fl2])
            V("act", "activation", dict(out=flv, in_=fl2v, func=AF.Ln, bias=1.0, scale=1.0), [R_fl2], [R_fl])
            k = nxt("pK", 3)
            k2 = nxt("pK", 3)
            mms([(pK[k][:, 0:NB * 16], triC_f, flv, True, True)], [R_fl, R_cst], [R_pK[k]])
            mms([(pK[k2][:, 0:NB * 16], ones_f, flv, True, True)], [R_fl, R_cst], [R_pK[k2]])
            tot = fl2
            V("dve", "tensor_copy", dict(out=fl2v, in_=pK[k2][:, 0:NB * 16]), [R_pK[k2]], [R_fl2])
            V("pool", "memset", dict(ap=carry[:, 0, :], constant=0.0), [], [R_carry])
            for bk in range(1, NB):
                V("dve", "tensor_tensor", dict(out=carry[:, bk, :], in0=carry[:, bk - 1, :],
                                                                  in1=tot[:, bk - 1, :], op=ALU.add), [R_carry, R_fl2], [R_carry])
            V("dve", "tensor_tensor", dict(out=negc[:, :, :].rearrange("p b h -> p (b h)"),
                                                       in0=pK[k][:, 0:NB * 16],
                                                       in1=carry[:, :, :].rearrange("p b h -> p (b h)"), op=ALU.add), [R_pK[k], R_carry], [R_negc])
            for g4 in range(NT):
                f = nxt("pF", 3)
                transposes([(pF[f][0:16, j * 128:(j + 1) * 128], negc[:, g4 * 4 + j, :], ident_f) for j in range(4)],
                           [R_negc, R_cst], [R_pF[f]])
                V("dve", "tensor_scalar", dict(
                    out=mrow[:, g4 * 512:(g4 + 1) * 512], in0=pF[f][0:16, :], scalar1=-8.0, scalar2=None,
                    op0=ALU.mult), [R_pF[f]], [R_mrow])
            dma("act", FQ[:, 64, :], mrow[:, :], reads=[R_mrow])
            S.barrier()

    if 0 in layers:
        phase_a0(x_in)
        heads0 = []
        for hh in range(8):
            heads0.append(dict(kind="sb", par=hh % 2, kd=64, scale=0.125,
                               q=[(SQK[hh // 2][(hh % 2) * 64:(hh % 2) * 64 + 64, :], 0, 64)],
                               k=[(SQK[4 + hh // 2][(hh % 2) * 64:(hh % 2) * 64 + 64, :], 0, 64)],
                               v=SV[:, hh * 64:(hh + 1) * 64], g=SG[hh // 2], o=OGT[hh // 2]))
        for hh in range(8):
            heads0.append(dict(kind="mla", par=hh % 2, kd=96, scale=float(96 ** -0.5),
                               q=[(MQ[hh], 0, 96)], k=[(MK[hh], 0, 64), (MKR, 64, 96)],
                               v=MV[:, hh * 64:(hh + 1) * 64], g=SG[4 + hh // 2], o=OGT[4 + hh // 2]))
        phase_attn(heads0, "b0")
        phase_c(0, x_in, X1 if 1 in layers else y_out)
    if 1 in layers:
        xs1 = X1 if 0 in layers else x_in
        phase_a1(xs1)
        heads1 = []
        for hh in range(16):
            heads1.append(dict(kind="fox", par=hh % 2, kd=65, scale=0.125, hidx=hh,
                               q=[(FQ[hh], 0, 65)], k=[(FK[hh], 0, 64)],
                               v=FV[:, hh * 64:(hh + 1) * 64], g=FG[hh // 2], o=OGT[hh // 2]))
        phase_attn(heads1, "b1")
        phase_c(1, xs1, y_out)
    S.emit()
    es.close()
    return nc


def make_consts():
    c = np.zeros((128, 1024), np.float32)
    i = np.arange(128)
    c[:, 0:128] = np.eye(128, dtype=np.float32)
    c[:, 128:256] = np.where(i[:, None] >= i[None, :], -8.0, 0.0)
    c[:, 256:384] = -8.0
    c[:, 384:512] = (i[:, None] <= i[None, :]).astype(np.float32)
    c[:, 512:640] = (i[:, None] < i[None, :]).astype(np.float32)
    c[:, 640:768] = (i[:, None] <= i[None, :]).astype(np.float32)
    c[:, 768:896] = 1.0
    return c


def perm_w0_in(w):
    return np.ascontiguousarray(np.concatenate(
        [w[:, 0:512], w[:, 512:1024], w[:, 1536:2048], w[:, 2720:3232], w[:, 1024:1536], w[:, 2048:2432],
         w[:, 2432:2720]], axis=1))


def perm_w_qb(w):
    w3 = w.reshape(384, 8, 96)
    return np.ascontiguousarray(np.concatenate([w3[:, :, 0:64].reshape(384, 512), w3[:, :, 64:96].reshape(384, 256)],
                                               axis=1))


def perm_w_kvb(w):
    w3 = w.reshape(256, 8, 128)
    return np.ascontiguousarray(np.concatenate([w3[:, :, 0:64].reshape(256, 512), w3[:, :, 64:128].reshape(256, 512)],
                                               axis=1))


def perm_w1_in(w):
    return np.ascontiguousarray(np.concatenate([w[:, 0:2048], w[:, 3072:4096], w[:, 2048:3072], w[:, 4096:4112]],
                                               axis=1))


_NC_CACHE = {}


def make_in_maps(inputs):
    f32 = lambda a: np.ascontiguousarray(np.asarray(a, dtype=np.float32))
    pcol = lambda a: np.ascontiguousarray(np.asarray(a, dtype=np.float32).reshape(-1, 128).T)
    shared = {
        "consts": make_consts(),
        "l0_pre_g": pcol(inputs["l0_pre_g"]), "l0_post_g": f32(inputs["l0_post_g"]),
        "l0_w_in": perm_w0_in(f32(inputs["l0_w_in"])),
        "l0_q_a_g": pcol(inputs["l0_q_a_g"]), "l0_w_q_b": perm_w_qb(f32(inputs["l0_w_q_b"])),
        "l0_kv_a_g": pcol(inputs["l0_kv_a_g"]), "l0_w_kv_b": perm_w_kvb(f32(inputs["l0_w_kv_b"])),
        "l0_w_out": f32(inputs["l0_w_out"]),
        "l1_pre_g": pcol(inputs["l1_pre_g"]), "l1_post_g": f32(inputs["l1_post_g"]),
        "l1_w_in": perm_w1_in(f32(inputs["l1_w_in"])), "l1_b_f": f32(inputs["l1_b_f"]),
        "l1_w_out": f32(inputs["l1_w_out"]),
    }
    x = f32(inputs["x"])
    pos = np.ascontiguousarray(np.asarray(inputs["positions"], dtype=np.int32))
    maps = []
    for b in range(8):
        m = dict(shared)
        m["x"] = np.ascontiguousarray(x[b])
        m["pos"] = np.ascontiguousarray(pos[b].reshape(NB, 128).T)
        maps.append(m)
    return maps


def kernel(**inputs):
    if "nc" not in _NC_CACHE:
        _NC_CACHE["nc"] = build_nc()
    nc = _NC_CACHE["nc"]
    in_maps = make_in_maps(inputs)
    res = run_bass_kernel_spmd(nc, in_maps, core_ids=list(range(8)))
    out = np.stack([np.asarray(r["y"], dtype=np.float32) for r in res.results], axis=0)
    return out
```
